# Optimizing a Trainium2 kernel written in Bass

```python
import jax, jax.numpy as jnp
from jax import lax
import numpy as np

D_MODEL = 1024
BATCH = 8
SEQ = 2048
DEPTH = 2
DEC_BATCH = 128
DEC_SEQ = 1
PAST_LEN = 16384
PAGE_SIZE = 128

N_META = 16
HEAD_DIM = 64
D_R = D_MODEL
N_HEADS = D_R // HEAD_DIM
D_C = D_MODEL
CONV_W = 3
LORA_W = 64
LORA_A = 64
LORA_G = 128
D_SHIFT = 3 * D_R + LORA_W + LORA_A + LORA_G
D_PROJ = D_SHIFT + 3 * D_C + 2 * D_MODEL
D_FF = 2816
RMS_EPS = 1e-6
GN_EPS = 64e-5

kernel_name = 'rwkv7_shortconv_macaron_hybrid_step'


def _rmsnorm(x, g):
    xf = x.astype(jnp.float32)
    y = xf * lax.rsqrt(jnp.mean(jnp.square(xf), axis=-1, keepdims=True) + RMS_EPS)
    return (y * g.astype(jnp.float32)).astype(x.dtype)


def _swiglu(x, w_in, w_out):
    gate, up = jnp.split(x @ w_in, 2, axis=-1)
    return (jax.nn.silu(gate) * up) @ w_out


def _wkv_step(S, inp):
    r_t, w_t, k_t, v_t, a_t, b_t = inp
    Sa = jnp.einsum('bhvk,bhk->bhv', S, a_t)
    S = S * w_t[:, :, None, :] + Sa[..., None] * b_t[:, :, None, :] + v_t[..., None] * k_t[:, :, None, :]
    y = jnp.einsum('bhvk,bhk->bhv', S, r_t)
    return S, y


def _mixer(xn, wkv0, shift0, conv0, w_in, mu_shift, w0, w_w2, a0, w_a2, w_g2,
           k_k, k_a, r_k, lnx_w, lnx_b, conv_w, w_o):
    f32 = jnp.float32
    bsz, L, _ = xn.shape
    proj = xn @ w_in
    ps, pc, pg = jnp.split(proj, [D_SHIFT, D_SHIFT + 3 * D_C], axis=-1)

    prev = jnp.concatenate([shift0[:, None, :].astype(ps.dtype), ps[:, :-1]], axis=1)
    xs = (ps + (prev - ps) * mu_shift).astype(f32)
    new_shift = ps[:, -1].astype(shift0.dtype)
    r, k, v, lw, la, lg = jnp.split(
        xs, [D_R, 2 * D_R, 3 * D_R, 3 * D_R + LORA_W, 3 * D_R + LORA_W + LORA_A], axis=-1)
    w_log = -jax.nn.softplus(-(w0.astype(f32) + jnp.tanh(lw) @ w_w2.astype(f32))) - 0.5
    decay = jnp.exp(-jnp.exp(w_log))
    a = jax.nn.sigmoid(a0.astype(f32) + la @ w_a2.astype(f32))
    g = jax.nn.sigmoid(lg) @ w_g2.astype(f32)

    def hs(t):
        return t.reshape(bsz, L, N_HEADS, HEAD_DIM)

    kk = hs(k * k_k.astype(f32))
    kk = kk / jnp.maximum(jnp.sqrt(jnp.sum(jnp.square(kk), axis=-1, keepdims=True)), 1e-12)
    kh = hs(k * (1.0 + (a - 1.0) * k_a.astype(f32)))
    ah, rh, vh, wh = hs(a), hs(r), hs(v), hs(decay)
    seq_in = tuple(jnp.moveaxis(t, 1, 0) for t in (rh, wh, kh, vh, -kk, kk * ah))
    S_T, y = lax.scan(_wkv_step, wkv0.astype(f32), seq_in)
    y = jnp.moveaxis(y, 0, 1)
    mu = jnp.mean(y, axis=-1, keepdims=True)
    var = jnp.mean(jnp.square(y - mu), axis=-1, keepdims=True)
    yn = ((y - mu) * lax.rsqrt(var + GN_EPS)).reshape(bsz, L, D_R) * lnx_w.astype(f32) + lnx_b.astype(f32)
    bonus = (jnp.sum(rh * kh * r_k.astype(f32), axis=-1, keepdims=True) * vh).reshape(bsz, L, D_R)
    y_a = ((yn + bonus) * g).astype(xn.dtype)

    gb, gc, hc = jnp.split(pc, 3, axis=-1)
    u = gc * hc
    u_pad = jnp.concatenate([conv0.astype(u.dtype), u], axis=1)
    z = conv_w[0] * u_pad[:, 0:L] + conv_w[1] * u_pad[:, 1:L + 1] + conv_w[2] * u_pad[:, 2:L + 2]
    new_conv = u_pad[:, -(CONV_W - 1):].astype(conv0.dtype)
    y_b = gb * z

    gate_a, gate_b = jnp.split(pg, 2, axis=-1)
    m = jax.nn.sigmoid(gate_a) * y_a + jax.nn.sigmoid(gate_b) * y_b
    return m @ w_o, S_T.astype(wkv0.dtype), new_shift, new_conv


def _trunk(x, wkv_in, shift_in, conv_in, p):
    new_wkv, new_shift, new_conv = [], [], []
    for l in range(DEPTH):
        x = x + 0.5 * _swiglu(_rmsnorm(x, p['norm_ffn1'][l]), p['ffn1_w_in'][l], p['ffn1_w_out'][l])
        h, s_w, s_s, s_c = _mixer(
            _rmsnorm(x, p['norm_mix'][l]), wkv_in[l], shift_in[l], conv_in[l],
            p['w_in'][l], p['mu_shift'][l], p['w0'][l], p['w_w2'][l], p['a0'][l], p['w_a2'][l],
            p['w_g2'][l], p['k_k'][l], p['k_a'][l], p['r_k'][l], p['lnx_w'][l], p['lnx_b'][l],
            p['conv_w'][l], p['w_o'][l])
        x = x + h
        x = x + 0.5 * _swiglu(_rmsnorm(x, p['norm_ffn2'][l]), p['ffn2_w_in'][l], p['ffn2_w_out'][l])
        new_wkv.append(s_w)
        new_shift.append(s_s)
        new_conv.append(s_c)
    y = _rmsnorm(x, p['norm_final'])
    return y, jnp.stack(new_wkv), jnp.stack(new_shift), jnp.stack(new_conv)


def setup_inputs(seed: int = 0) -> dict:
    key = jax.random.key(seed)
    ks = jax.random.split(key, 28)

    def nrm(k, shape, s):
        return jax.random.normal(k, shape, jnp.float32) * s

    return {
        'x_prompt': nrm(ks[0], (BATCH, SEQ, D_MODEL), 1.0),
        'x_sample': nrm(ks[1], (DEC_BATCH, DEC_SEQ, D_MODEL), 1.0),
        'state_wkv': nrm(ks[2], (DEPTH, DEC_BATCH, N_HEADS, HEAD_DIM, HEAD_DIM), 0.3),
        'state_shift': nrm(ks[3], (DEPTH, DEC_BATCH, D_SHIFT), 1.0),
        'state_conv': nrm(ks[4], (DEPTH, DEC_BATCH, CONV_W - 1, D_C), 1.0),
        'meta_tokens': nrm(ks[5], (N_META, D_MODEL), 1.0),
        'norm_ffn1': 1.0 + nrm(ks[6], (DEPTH, D_MODEL), 0.02),
        'ffn1_w_in': nrm(ks[7], (DEPTH, D_MODEL, 2 * D_FF), D_MODEL ** -0.5),
        'ffn1_w_out': nrm(ks[8], (DEPTH, D_FF, D_MODEL), D_FF ** -0.5),
        'norm_mix': 1.0 + nrm(ks[9], (DEPTH, D_MODEL), 0.02),
        'w_in': nrm(ks[10], (DEPTH, D_MODEL, D_PROJ), D_MODEL ** -0.5),
        'mu_shift': jax.random.uniform(ks[11], (DEPTH, D_SHIFT), jnp.float32),
        'w0': jax.random.uniform(ks[12], (DEPTH, D_R), jnp.float32, minval=-4.0, maxval=1.0),
        'w_w2': nrm(ks[13], (DEPTH, LORA_W, D_R), 0.1 * LORA_W ** -0.5),
        'a0': nrm(ks[14], (DEPTH, D_R), 0.1),
        'w_a2': nrm(ks[15], (DEPTH, LORA_A, D_R), 0.5 * LORA_A ** -0.5),
        'w_g2': nrm(ks[16], (DEPTH, LORA_G, D_R), LORA_G ** -0.5),
        'k_k': 0.85 + nrm(ks[17], (DEPTH, D_R), 0.02),
        'k_a': 1.0 + nrm(ks[18], (DEPTH, D_R), 0.02),
        'r_k': nrm(ks[19], (DEPTH, N_HEADS, HEAD_DIM), 0.1),
        'lnx_w': 1.0 + nrm(ks[20], (DEPTH, D_R), 0.02),
        'lnx_b': nrm(ks[21], (DEPTH, D_R), 0.02),
        'conv_w': nrm(ks[22], (DEPTH, CONV_W, D_C), CONV_W ** -0.5),
        'w_o': nrm(ks[23], (DEPTH, D_MODEL, D_MODEL), D_MODEL ** -0.5),
        'norm_ffn2': 1.0 + nrm(ks[24], (DEPTH, D_MODEL), 0.02),
        'ffn2_w_in': nrm(ks[25], (DEPTH, D_MODEL, 2 * D_FF), D_MODEL ** -0.5),
        'ffn2_w_out': nrm(ks[26], (DEPTH, D_FF, D_MODEL), D_FF ** -0.5),
        'norm_final': 1.0 + nrm(ks[27], (D_MODEL,), 0.02),
    }


def reference(x_prompt, x_sample, state_wkv, state_shift, state_conv, meta_tokens,
              norm_ffn1, ffn1_w_in, ffn1_w_out, norm_mix, w_in, mu_shift, w0, w_w2, a0,
              w_a2, w_g2, k_k, k_a, r_k, lnx_w, lnx_b, conv_w, w_o, norm_ffn2,
              ffn2_w_in, ffn2_w_out, norm_final):
    p = dict(norm_ffn1=norm_ffn1, ffn1_w_in=ffn1_w_in, ffn1_w_out=ffn1_w_out,
             norm_mix=norm_mix, w_in=w_in, mu_shift=mu_shift, w0=w0, w_w2=w_w2, a0=a0,
             w_a2=w_a2, w_g2=w_g2, k_k=k_k, k_a=k_a, r_k=r_k, lnx_w=lnx_w, lnx_b=lnx_b,
             conv_w=conv_w, w_o=w_o, norm_ffn2=norm_ffn2, ffn2_w_in=ffn2_w_in,
             ffn2_w_out=ffn2_w_out, norm_final=norm_final)

    bp = x_prompt.shape[0]
    dt = x_prompt.dtype
    meta = jnp.broadcast_to(meta_tokens.astype(dt)[None], (bp, N_META, D_MODEL))
    xp = jnp.concatenate([meta, x_prompt], axis=1)
    wkv0 = jnp.zeros((DEPTH, bp, N_HEADS, HEAD_DIM, HEAD_DIM), dt)
    shift0 = jnp.zeros((DEPTH, bp, D_SHIFT), dt)
    conv0 = jnp.zeros((DEPTH, bp, CONV_W - 1, D_C), dt)
    yp, p_wkv, p_shift, p_conv = _trunk(xp, wkv0, shift0, conv0, p)
    y_prompt = yp[:, N_META:]

    y_sample, s_wkv, s_shift, s_conv = _trunk(x_sample, state_wkv, state_shift, state_conv, p)

    return (y_prompt, y_sample, p_wkv, p_shift, p_conv, s_wkv, s_shift, s_conv)
```

```python
import numpy as np
from contextlib import ExitStack
import concourse.bass as bass
import concourse.mybir as mybir
from concourse.bass_utils import run_bass_kernel_spmd

F32 = mybir.dt.float32
F32R = mybir.dt.float32r
BF16 = mybir.dt.bfloat16
AF = mybir.ActivationFunctionType
ALU = mybir.AluOpType
AX = mybir.AxisListType

D = 1024
DFF = 2816
DPROJ = 8448
DSH = 3328
NL = 2
NSAMP = 16
CW = 0.6065306597126334
TW = 320
SEQ = 2048
WMAX = TW
NSLOT = 6
NPF = 2
PVL = 131
NV = 2 * PVL + 8
O_NF1, O_NMIX, O_NF2, O_MU, O_W0, O_A0, O_KK, O_KA, O_RK, O_LNW, O_LNB, O_CW = 0, 8, 16, 24, 51, 59, 67, 75, 83, 91, 99, 107
C_ID, C_ON, C_OBD, C_OBD64, C_MALL, C_I64, C_RST = 0, 128, 256, 384, 512, 1024, 1088
NCST = 1408
PIECES = [(128 * i, 128) for i in range(24)] + [(3072, 64), (3136, 64), (3200, 128)]


class Prog:
    ENGS = ["pe", "act", "dve", "pool", "sp"]

    def __init__(self, nc):
        self.nc = nc
        self.streams = {e: [] for e in self.ENGS}
        self.count = {}
        self.res = {}
        self.seen = {e: {} for e in self.ENGS}
        self.semkeys = []
        self.ndma = 0

    def _tok(self, key, amt):
        if key not in self.count:
            self.count[key] = 0
            self.semkeys.append(key)
        self.count[key] += amt
        return (key, self.count[key])

    def begin(self):
        self.rec = []

    def end(self):
        r, self.rec = self.rec, None
        return r

    def play(self, lists):
        lists = [l for l in lists if l]
        pos = [0] * len(lists)
        total = sum(len(l) for l in lists)
        for step in range(total):
            best, bi = None, None
            for i, l in enumerate(lists):
                if pos[i] < len(l):
                    frac = pos[i] / len(l)
                    if best is None or frac < best:
                        best, bi = frac, i
            a = lists[bi][pos[bi]]
            pos[bi] += 1
            self.op(a[0], a[1], reads=a[2], writes=a[3], dma=a[4])

    def op(self, eng, fn, reads=(), writes=(), dma=None):
        if getattr(self, "rec", None) is not None:
            self.rec.append((eng, fn, tuple(reads), tuple(writes), dma))
            return None
        waits = {}

        def need(tok):
            if tok is None:
                return
            k, v = tok
            if eng == "pe" and k == "pe":
                return
            if self.seen[eng].get(k, 0) >= v:
                return
            waits[k] = max(waits.get(k, 0), v)

        for r in reads:
            st = self.res.get(r)
            if st:
                need(st["w"])
        for w in writes:
            st = self.res.get(w)
            if st:
                need(st["w"])
                for t in st["r"]:
                    need(t)
        for k, v in waits.items():
            self.seen[eng][k] = v
        tok = self._tok(dma, 16) if dma else self._tok(eng, 1)
        for r in reads:
            st = self.res.setdefault(r, {"w": None, "r": []})
            st["r"].append(tok)
        for w in writes:
            self.res[w] = {"w": tok, "r": []}
        self.streams[eng].append((fn, list(waits.items()), tok, 16 if dma else 1))
        return tok

    def dma(self, eng, out, in_, reads=(), writes=(), slow=False, key=None):
        key = "dma_" + (key if key else str((writes or reads)[0]))
        if slow:
            f = lambda e, o=out, i=in_: e.dma_start(out=o, in_=i, allow_slow_non_contiguous=True)
        else:
            f = lambda e, o=out, i=in_: e.dma_start(out=o, in_=i)
        return self.op(eng, f, reads=reads, writes=writes, dma=key)

    def seal(self, key, resources):
        tok = ("dma_" + key, self.count["dma_" + key])
        for r in resources:
            if r in self.res:
                self.res[r]["w"] = tok

    def emit(self):
        nc = self.nc
        with ExitStack() as es:
            sems = {k: es.enter_context(nc.semaphore(f"s_{i}")) for i, k in enumerate(self.semkeys)}
            block = es.enter_context(nc.Block())

            def mk(engname):
                def body(e):
                    for fn, waits, tok, amt in self.streams[engname]:
                        for k, v in waits:
                            e.wait_ge(sems[k], v)
                        inst = fn(e)
                        inst.then_inc(sems[tok[0]], amt)
                    if engname == "sp":
                        for k in self.semkeys:
                            e.wait_ge(sems[k], self.count[k])
                return body

            block.tensor(mk("pe"))
            block.scalar(mk("act"))
            block.vector(mk("dve"))
            block.gpsimd(mk("pool"))
            block.sync(mk("sp"))


def build_program(SEQ=SEQ, NL=NL):
    total = SEQ + 64
    nfull = (total - 1) // TW
    rem = total - nfull * TW
    assert rem % 64 == 0 and 64 <= rem and rem + NSAMP <= TW
    TILES = [(TW, TW // 64, 0)] * nfull + [(rem + NSAMP, rem // 64, NSAMP)]
    nc = bass.Bass("TRN2", target_bir_lowering=False)
    es = ExitStack()

    def din(name, shape):
        return nc.dram_tensor(name, list(shape), F32, kind="ExternalInput").ap()

    def dout(name, shape):
        return nc.dram_tensor(name, list(shape), F32, kind="ExternalOutput").ap()

    xp = din("xp", [SEQ, D]); xsm = din("xsm", [NSAMP, D]); meta = din("meta", [16, D])
    swkv = din("swkv", [NL, NSAMP, 16, 64, 64]); sshift = din("sshift", [NL, NSAMP, DSH]); sconv = din("sconv", [NL, NSAMP, 2, D])
    f1i = din("f1i", [NL, D, 2 * DFF]); f1o = din("f1o", [NL, DFF, D]); wi = din("wi", [NL, D, DPROJ]); wo = din("wo", [NL, D, D])
    f2i = din("f2i", [NL, D, 2 * DFF]); f2o = din("f2o", [NL, DFF, D])
    ww2 = din("ww2", [NL, 64, D]); wa2 = din("wa2", [NL, 64, D]); wg2 = din("wg2", [NL, 128, D])
    pvd = din("pv", [128, NV]); cstd = din("cst", [128, NCST])
    yp = dout("yp", [SEQ, D]); ys = dout("ys", [NSAMP, D])
    owkv_p = dout("owkv_p", [NL, 16, 64, 64]); oshift_p = dout("oshift_p", [NL, DSH]); oconv_p = dout("oconv_p", [NL, 2, D])
    owkv_s = dout("owkv_s", [NL, NSAMP, 16, 64, 64]); oshift_s = dout("oshift_s", [NL, NSAMP, DSH]); oconv_s = dout("oconv_s", [NL, NSAMP, 2, D])

    def sb(name, shape, dt=F32):
        return es.enter_context(nc.sbuf_tensor(name, list(shape), dt))

    W = WMAX
    xT = sb("xT", [128, 8, W]); xn = sb("xn", [128, 8, W], BF16)
    big = sb("big", [128, 22, W], BF16)
    xs = sb("xs", [128, 12, W], BF16)
    wslots = [sb(f"ws{i}", [128, 4096], BF16) for i in range(NSLOT)]
    pv = sb("pvt", [128, NV]); cst = sb("cstt", [128, NCST]); onesNb = sb("onesNb", [128, 128], BF16)
    omka = sb("omka", [128, NL, 8]); identb = sb("identb", [128, 128], BF16)
    xtok = sb("xtok", [128, D]); ytok = xtok
    rstd = sb("rstd", [128, W]); tmpA = sb("tmpA", [128, W]); tmpB = sb("tmpB", [128, W])
    psb = [sb(f"psb{i}", [128, W + 1]) for i in range(2)]
    carry = sb("carry", [128, NL, 27]); ucarry = sb("ucarry", [128, NL, 8, 2])
    sshT = sb("sshT", [128, NL, 27, NSAMP]); scvT = sb("scvT", [128, NL, 8, 2, NSAMP])
    oshT = sb("oshT", [128, NL, 27, NSAMP]); ocvT = sb("ocvT", [128, NL, 8, NSAMP])
    tlw = sb("tlw", [64, W], BF16); lat = sb("lat", [64, W], BF16); slg = sb("slg", [128, W], BF16)
    ubuf = [sb(f"ubuf{i}", [128, W + 2]) for i in range(4)]
    ybuf = [sb(f"ybuf{i}", [128, W], BF16) for i in range(4)]
    sga = [sb(f"sga{i}", [128, W], BF16) for i in range(4)]
    PTS = []
    for i in range(NPF):
        d = {nm: sb(f"pt{i}_{nm}", [128, W]) for nm in ["sw", "cs", "E2", "E3", "a", "kk", "kh", "t2"]}
        d["f0"] = sb(f"pt{i}_f0", [128, W])
        d["btn"] = sb(f"pt{i}_btn", [128, W], BF16)
        PTS.append(d)
    NCH = W // 64
    PB = []
    for i in range(NPF):
        d = {}
        for nm in ["E1", "bonus", "g"]:
            d[nm] = sb(f"p{i}_{nm}", [128, W])
        d["f0"] = sb(f"p{i}_f0", [128, W])
        d["yT"] = sb(f"p{i}_yT", [128, W])
        d["smp"] = sb(f"p{i}_smp", [128, 4, NSAMP])
        d["lora"] = sb(f"p{i}_lora", [128, 3, 128], BF16)
        d["AR"] = sb(f"p{i}_AR", [128, NCH, 192], BF16)
        d["BDb"] = sb(f"p{i}_BDb", [128, NCH, 128], BF16)
        d["BDk"] = sb(f"p{i}_BDk", [128, NCH, 128], BF16)
        d["BDv"] = sb(f"p{i}_BDv", [128, NCH, 128], BF16)
        d["U"] = sb(f"p{i}_U", [128, 2, 128], BF16)
        PB.append(d)
    NU = NPF * NCH
    arena = sb("arena", [128, NU * 512], F32)
    UB = arena[:].bitcast(BF16).rearrange("p (u c) -> p u c", c=1024)
    UBK = [f"ub{u}" for u in range(NU)]
    Sf = sb("Sf", [128, NL, 8, 128]); Sb = sb("Sbf", [128, NL, 8, 128], BF16)
    assert NU * 512 >= 7 * 512
    _ab = lambda i: arena[:, 512 * i:512 * (i + 1)].rearrange("p (b k) -> p b k", k=64)
    SBUFS = {"sst": [_ab(0), _ab(1)], "dxp": [_ab(2), _ab(3)], "st1": _ab(4), "st2": [_ab(5), _ab(6)]}
    fence_t = sb("fence_t", [128, 2]); sa_t = sb("sa_t", [128, NSAMP]); ys_t = sb("ys_t", [128, NSAMP]); fin = sb("fin", [128, 128])
    PS = [es.enter_context(nc.psum_tensor(f"ps{i}", [128, 512], F32)) for i in range(8) if i != 4]
    PS.insert(4, es.enter_context(nc.psum_tensor("ps4", [128, 1024], BF16)))
    asf = lambda ap: ap.bitcast(F32)

    P = Prog(nc)
    op = P.op

    P.dma("sp", pv[:], pvd[:, :], writes=["pv"])
    P.dma("sp", cst[:], cstd[:, :], writes=["cst"])
    op("dve", lambda e: e.tensor_copy(out=identb[:], in_=cst[:, C_ID:C_ID + 128]), reads=["cst"], writes=["identb"])
    op("dve", lambda e: e.tensor_copy(out=onesNb[:], in_=cst[:, C_ON:C_ON + 128]), reads=["cst"], writes=["onesNb"])
    for l in range(NL):
        op("dve", lambda e, l=l: e.tensor_scalar(out=omka[:, l, :], in0=pv[:, l * PVL + O_KA:l * PVL + O_KA + 8], scalar1=-1.0, scalar2=1.0, op0=ALU.mult, op1=ALU.add),
           reads=["pv"], writes=[f"omka{l}"])
    op("dve", lambda e: e.memset(carry[:], 0.0), writes=["carry"])
    op("dve", lambda e: e.memset(ucarry[:], 0.0), writes=["ucarry"])
    op("dve", lambda e: e.memset(Sf[:], 0.0), writes=["Sf"])
    op("dve", lambda e: e.memset(Sb[:], 0.0), writes=["Sb"])
    for i in range(NPF):
        for nm in ["AR", "BDb", "BDk", "BDv"]:
            op("dve", lambda e, t=PB[i][nm]: e.memset(t[:], 0.0), writes=[f"p{i}_{nm}"])
    setup_res = []
    for l in range(NL):
        for pi, (c0, R) in enumerate(PIECES):
            P.dma("sp", sshT[0:R, l, pi, :], sshift[l, :, c0:c0 + R].rearrange("b f -> f b"), writes=[f"sshT{l}_{pi}"], slow=True, key="setup")
            setup_res.append(f"sshT{l}_{pi}")
        for c in range(8):
            for j in range(2):
                P.dma("sp", scvT[:, l, c, j, :], sconv[l, :, j, c * 128:(c + 1) * 128].rearrange("b f -> f b"), writes=[f"scvT{l}_{c}_{j}"], slow=True, key="setup")
                setup_res.append(f"scvT{l}_{c}_{j}")
    P.seal("setup", setup_res)

    def pvc(l, off, c):
        return pv[:, l * PVL + off + c:l * PVL + off + c + 1]

    ident = cst[:, C_ID:C_ID + 128]
    obd_f = cst[:, C_OBD:C_OBD + 128]; obd_r = obd_f; obd64_r = cst[:, C_OBD64:C_OBD64 + 128]
    i64s = cst[:, C_I64:C_I64 + 64]

    wstate = {"n": 0, "j": 0, "tile": 0}
    NBLK = 59 * NL
    wc = nc.dram_tensor("wcache", [NBLK, 128, 4096], BF16, kind="Internal").ap()

    def wload(src_ap, shape_str, **kw):
        i = wstate["n"] % NSLOT
        wstate["n"] += 1
        j = wstate["j"]
        wstate["j"] += 1
        t = wslots[i]
        n = 1
        for v in src_ap.shape:
            n *= v
        per = n // 128
        kk_ = 8 if shape_str == "in" else 22
        view = t[:, 0:per].rearrange("p (k n) -> p k n", k=kk_)
        if wstate["tile"] == 0:
            P.dma("pool", view, src_ap.rearrange("(k p) n -> p k n", p=128), writes=[f"ws{i}"])
            P.dma("sp", wc[j, :, 0:per], t[:, 0:per], reads=[f"ws{i}"], writes=[f"wc{j}"], key=f"wst{i}")
        else:
            P.dma("pool", t[:, 0:per], wc[j, :, 0:per], reads=[f"wc{j}"], writes=[f"ws{i}"])
        return (i, view)

    mmtoggle = {"n": 0}

    def mmbank():
        mmtoggle["n"] ^= 1
        return mmtoggle["n"]

    def proj(slot, wv, col, M, rhs_fn, Wt, nk, rkeys):
        b = mmbank()
        pt = PS[b]
        for k in range(nk):
            op("pe", lambda e, k=k, pt=pt: e.matmul(pt[0:M, 0:Wt], wv[:, k, col:col + M], rhs_fn(k), start=(k == 0), stop=(k == nk - 1)),
               reads=[f"ws{slot}"] + rkeys, writes=[f"ps{b}"])
        return b, pt[0:M, 0:Wt]

    def rmsnorm(l_off_col_fn, Wt, tag, final=False):
        for c in range(8):
            op("act", lambda e, c=c: e.activation(out=xn[:, c, 0:Wt], in_=xT[:, c, 0:Wt], func=AF.Square), reads=[f"xT{c}"], writes=[f"xn{c}"])
        for c in range(8):
            op("pe", lambda e, c=c: e.matmul(PS[2][:, 0:Wt], onesNb[:], xn[:, c, 0:Wt], start=(c == 0), stop=(c == 7)),
               reads=[f"xn{c}", "onesNb"], writes=["ps2"])
        op("act", lambda e: e.activation(out=rstd[:, 0:Wt], in_=PS[2][:, 0:Wt], func=AF.Sqrt, bias=eps_t[:, 0:1]), reads=["ps2", "eps"], writes=["rstd"])
        op("dve", lambda e: e.reciprocal(out=rstd[:, 0:Wt], in_=rstd[:, 0:Wt]), reads=["rstd"], writes=["rstd"])
        for c in range(8):
            dst = xT if final else xn
            op("dve", lambda e, c=c, dst=dst: e.scalar_tensor_tensor(out=dst[:, c, 0:Wt], in0=xT[:, c, 0:Wt], scalar=l_off_col_fn(c), in1=rstd[:, 0:Wt], op0=ALU.mult, op1=ALU.mult),
               reads=[f"xT{c}", "rstd", "pv"], writes=[f"xT{c}" if final else f"xn{c}"])

    eps_t = sb("eps_t", [128, 4])
    op("dve", lambda e: e.memset(eps_t[:, 0:1], 1e-6), writes=["eps"])
    op("dve", lambda e: e.memset(eps_t[:, 1:2], 64e-5), writes=["eps"])
    op("dve", lambda e: e.memset(eps_t[:, 2:3], 1e-30), writes=["eps"])

    def ffn(l, w_in, w_out, Wt):
        for q in range(6):
            ncol = 512 if q < 5 else 256
            sg, wg = wload(w_in[l, :, 512 * q:512 * q + ncol], "in")
            su, wu = wload(w_in[l, :, DFF + 512 * q:DFF + 512 * q + ncol], "in")
            for jj in range(ncol // 128):
                j = 4 * q + jj
                bg, pg = proj(sg, wg, jj * 128, 128, lambda k: xn[:, k, 0:Wt], Wt, 8, [f"xn{k}" for k in range(8)])
                op("act", lambda e, pg=pg: e.activation(out=tmpA[:, 0:Wt], in_=pg, func=AF.Silu), reads=[f"ps{bg}"], writes=["tmpA"])
                bu, pu = proj(su, wu, jj * 128, 128, lambda k: xn[:, k, 0:Wt], Wt, 8, [f"xn{k}" for k in range(8)])
                op("dve", lambda e, pu=pu, j=j: e.tensor_tensor(out=big[:, j, 0:Wt], in0=pu, in1=tmpA[:, 0:Wt], op=ALU.mult),
                   reads=[f"ps{bu}", "tmpA"], writes=[f"big{j}"])
        for oc in range(8):
            so, wv = wload(w_out[l, :, oc * 128:(oc + 1) * 128], "out")
            b, po = proj(so, wv, 0, 128, lambda k: big[:, k, 0:Wt], Wt, 22, [f"big{k}" for k in range(22)])
            op("dve", lambda e, po=po, oc=oc: e.scalar_tensor_tensor(out=xT[:, oc, 0:Wt], in0=po, scalar=0.5, in1=xT[:, oc, 0:Wt], op0=ALU.mult, op1=ALU.add),
               reads=[f"ps{b}", f"xT{oc}"], writes=[f"xT{oc}"])

    psbt = {"n": 0}

    def shiftmix(l, pi, b, pp, nseq, nsamp, out_ap_fn, okey, last_tile):
        c0, R = PIECES[pi]
        Wt = nseq + nsamp
        psbt["n"] ^= 1
        bi = psbt["n"]
        pb = psb[bi]
        op("act", lambda e: e.activation(out=pb[0:R, 0:1], in_=carry[0:R, l, pi:pi + 1], func=AF.Copy), reads=[f"carry{l}_{pi}", "carry"], writes=[f"psb{bi}"])
        op("act", lambda e: e.activation(out=pb[0:R, 1:Wt + 1], in_=pp, func=AF.Copy), reads=[f"ps{b}"], writes=[f"psb{bi}"])
        op("act", lambda e: e.activation(out=carry[0:R, l, pi:pi + 1], in_=pb[0:R, nseq:nseq + 1], func=AF.Copy), reads=[f"psb{bi}"], writes=[f"carry{l}_{pi}"])
        mu = pv[0:R, l * PVL + O_MU + pi:l * PVL + O_MU + pi + 1]
        op("dve", lambda e: e.tensor_tensor(out=tmpB[0:R, 0:nseq], in0=pb[0:R, 0:nseq], in1=pb[0:R, 1:nseq + 1], op=ALU.subtract), reads=[f"psb{bi}"], writes=["tmpB"])
        o = out_ap_fn()
        op("dve", lambda e: e.scalar_tensor_tensor(out=o[:, 0:nseq], in0=tmpB[0:R, 0:nseq], scalar=mu, in1=pb[0:R, 1:nseq + 1], op0=ALU.mult, op1=ALU.add),
           reads=["tmpB", f"psb{bi}", "pv"], writes=[okey])
        if nsamp:
            op("dve", lambda e: e.tensor_tensor(out=tmpB[0:R, nseq:Wt], in0=sshT[0:R, l, pi, :], in1=pb[0:R, 1 + nseq:1 + Wt], op=ALU.subtract),
               reads=[f"psb{bi}", f"sshT{l}_{pi}"], writes=["tmpB"])
            op("dve", lambda e: e.scalar_tensor_tensor(out=o[:, nseq:Wt], in0=tmpB[0:R, nseq:Wt], scalar=mu, in1=pb[0:R, 1 + nseq:1 + Wt], op0=ALU.mult, op1=ALU.add),
               reads=["tmpB", f"psb{bi}", "pv"], writes=[okey])
            op("act", lambda e: e.activation(out=oshT[0:R, l, pi, :], in_=pb[0:R, 1 + nseq:1 + Wt], func=AF.Copy), reads=[f"psb{bi}"], writes=[f"oshT{l}_{pi}"])
            P.dma("sp", oshift_s[l, :, c0:c0 + R].rearrange("b f -> f b"), oshT[0:R, l, pi, :], reads=[f"oshT{l}_{pi}"], slow=True, key="out")
        if last_tile:
            P.dma("sp", oshift_p[l, c0:c0 + R].rearrange("(f o) -> f o", o=1), carry[0:R, l, pi:pi + 1], reads=[f"carry{l}_{pi}"], slow=True, key="out")

    def wkv_prep(l, c, pi_, xi, nseq, nsamp):
        B = PB[pi_]
        PSs = PS[2] if pi_ == 0 else PS[7]
        psk = "ps2" if pi_ == 0 else "ps7"
        Wt = nseq + nsamp
        nch = nseq // 64
        k_ = lambda nm: f"p{pi_}_{nm}"
        t_ = lambda nm: f"pt{pi_}_{nm}"
        PT = PTS[pi_]
        xr = xs[:, xi, 0:Wt]; xk = xs[:, 4 + xi, 0:Wt]; xv = xs[:, 8 + xi, 0:Wt]
        kr, kkk, kv = f"xs{xi}", f"xs{4 + xi}", f"xs{8 + xi}"
        cs_ = slice(c * 128, (c + 1) * 128)
        lo = B["lora"]
        P.dma("pool", lo[0:64, 0, :], ww2[l, :, cs_], writes=[k_("lora")], key=f"lora{pi_}")
        P.dma("pool", lo[0:64, 1, :], wa2[l, :, cs_], writes=[k_("lora")], key=f"lora{pi_}")
        P.dma("pool", lo[:, 2, :], wg2[l, :, cs_], writes=[k_("lora")], key=f"lora{pi_}")
        sw, cs, E2, E3, a, kk, kh, t2, f0, btn = (PT[n] for n in ["sw", "cs", "E2", "E3", "a", "kk", "kh", "t2", "f0", "btn"])
        op("pe", lambda e: e.matmul(PSs[:, 0:Wt], lo[0:64, 0, :], tlw[:, 0:Wt], start=True, stop=True), reads=["tlw", k_("lora")], writes=[psk])
        op("act", lambda e: e.activation(out=sw[:, 0:Wt], in_=PSs[:, 0:Wt], func=AF.Sigmoid, bias=pvc(l, O_W0, c)), reads=[psk, "pv"], writes=[t_("sw")])
        op("pe", lambda e: e.matmul(PSs[:, 0:Wt], lo[0:64, 1, :], lat[:, 0:Wt], start=True, stop=True), reads=["lat", k_("lora")], writes=[psk])
        op("act", lambda e: e.activation(out=a[:, 0:Wt], in_=PSs[:, 0:Wt], func=AF.Sigmoid, bias=pvc(l, O_A0, c)), reads=[psk, "pv"], writes=[t_("a")])
        op("dve", lambda e: e.tensor_tensor_scan(out=cs[:, 0:nseq], data0=cst[:, C_RST:C_RST + nseq], data1=sw[:, 0:nseq], initial=0.0, op0=ALU.mult, op1=ALU.add),
           reads=[t_("sw"), "cst"], writes=[t_("cs")])
        op("act", lambda e: e.activation(out=B["E1"][:, 0:nseq], in_=cs[:, 0:nseq], func=AF.Exp, scale=-CW), reads=[t_("cs")], writes=[k_("E1")])
        op("act", lambda e: e.activation(out=E2[:, 0:nseq], in_=cs[:, 0:nseq], func=AF.Exp, scale=CW), reads=[t_("cs")], writes=[t_("E2")])
        op("dve", lambda e: e.tensor_tensor(out=E3[:, 0:nseq], in0=cs[:, 0:nseq], in1=sw[:, 0:nseq], op=ALU.subtract), reads=[t_("cs"), t_("sw")], writes=[t_("E3")])
        op("act", lambda e: e.activation(out=E3[:, 0:nseq], in_=E3[:, 0:nseq], func=AF.Exp, scale=-CW), reads=[t_("E3")], writes=[t_("E3")])
        op("dve", lambda e: e.tensor_scalar(out=kk[:, 0:Wt], in0=xk, scalar1=pvc(l, O_KK, c), scalar2=None, op0=ALU.mult), reads=[kkk, "pv"], writes=[t_("kk")])
        op("act", lambda e: e.activation(out=f0[:, 0:Wt], in_=kk[:, 0:Wt], func=AF.Square), reads=[t_("kk")], writes=[t_("f0")])
        op("pe", lambda e: e.matmul(PSs[:, 0:Wt], obd_r, f0[:, 0:Wt], start=True, stop=True), reads=[t_("f0"), "cst"], writes=[psk])
        op("act", lambda e: e.activation(out=t2[:, 0:Wt], in_=PSs[:, 0:Wt], func=AF.Sqrt, bias=eps_t[:, 2:3]), reads=[psk, "eps"], writes=[t_("t2")])
        op("dve", lambda e: e.tensor_scalar(out=t2[:, 0:Wt], in0=t2[:, 0:Wt], scalar1=1e-12, scalar2=None, op0=ALU.max), reads=[t_("t2")], writes=[t_("t2")])
        op("dve", lambda e: e.reciprocal(out=t2[:, 0:Wt], in_=t2[:, 0:Wt]), reads=[t_("t2")], writes=[t_("t2")])
        op("dve", lambda e: e.tensor_tensor(out=kk[:, 0:Wt], in0=kk[:, 0:Wt], in1=t2[:, 0:Wt], op=ALU.mult), reads=[t_("kk"), t_("t2")], writes=[t_("kk")])
        op("dve", lambda e: e.tensor_scalar(out=t2[:, 0:Wt], in0=a[:, 0:Wt], scalar1=pvc(l, O_KA, c), scalar2=omka[:, l, c:c + 1], op0=ALU.mult, op1=ALU.add),
           reads=[t_("a"), "pv", f"omka{l}"], writes=[t_("t2")])
        op("dve", lambda e: e.tensor_tensor(out=kh[:, 0:Wt], in0=xk, in1=t2[:, 0:Wt], op=ALU.mult), reads=[kkk, t_("t2")], writes=[t_("kh")])
        op("dve", lambda e: e.tensor_tensor(out=a[:, 0:Wt], in0=kk[:, 0:Wt], in1=a[:, 0:Wt], op=ALU.mult), reads=[t_("kk"), t_("a")], writes=[t_("a")])
        op("dve", lambda e: e.scalar_tensor_tensor(out=f0[:, 0:Wt], in0=xr, scalar=pvc(l, O_RK, c), in1=kh[:, 0:Wt], op0=ALU.mult, op1=ALU.mult),
           reads=[kr, t_("kh"), "pv"], writes=[t_("f0")])
        op("pe", lambda e: e.matmul(PSs[:, 0:Wt], obd_r, f0[:, 0:Wt], start=True, stop=True), reads=[t_("f0"), "cst"], writes=[psk])
        op("dve", lambda e: e.tensor_tensor(out=B["bonus"][:, 0:Wt], in0=PSs[:, 0:Wt], in1=xv, op=ALU.mult), reads=[psk, kv], writes=[k_("bonus")])
        op("pe", lambda e: e.matmul(PSs[:, 0:Wt], lo[:, 2, :], slg[:, 0:Wt], start=True, stop=True), reads=["slg", k_("lora")], writes=[psk])
        op("act", lambda e: e.activation(out=B["g"][:, 0:Wt], in_=PSs[:, 0:Wt], func=AF.Copy), reads=[psk], writes=[k_("g")])
        v3 = lambda ap: ap.rearrange("p (c s) -> p c s", s=64)
        op("dve", lambda e: e.tensor_tensor(out=B["AR"][:, 0:nch, 0:64], in0=v3(xr[:, 0:nseq]), in1=v3(B["E1"][:, 0:nseq]), op=ALU.mult), reads=[kr, k_("E1")], writes=[k_("AR")])
        op("dve", lambda e: e.tensor_tensor(out=btn[:, 0:nseq], in0=a[:, 0:nseq], in1=E2[:, 0:nseq], op=ALU.mult), reads=[t_("a"), t_("E2")], writes=[t_("btn")])
        for h in range(2):
            ps_ = slice(64 * h, 64 * h + 64)
            op("dve", lambda e, ps_=ps_, h=h: e.scalar_tensor_tensor(out=B["AR"][ps_, 0:nch, 64 + 64 * h:128 + 64 * h], in0=v3(kk[ps_, 0:nseq]), scalar=-1.0, in1=v3(E3[ps_, 0:nseq]), op0=ALU.mult, op1=ALU.mult),
               reads=[t_("kk"), t_("E3")], writes=[k_("AR")])
            op("act", lambda e, ps_=ps_: e.activation(out=B["BDb"][ps_, 0:nch, ps_], in_=v3(btn[ps_, 0:nseq]), func=AF.Copy), reads=[t_("btn")], writes=[k_("BDb")])
            op("dve", lambda e, ps_=ps_: e.tensor_tensor(out=B["BDk"][ps_, 0:nch, ps_], in0=v3(kh[ps_, 0:nseq]), in1=v3(E2[ps_, 0:nseq]), op=ALU.mult),
               reads=[t_("kh"), t_("E2")], writes=[k_("BDk")])
            op("act", lambda e, ps_=ps_: e.activation(out=B["BDv"][ps_, 0:nch, ps_], in_=v3(xv[ps_, 0:nseq]), func=AF.Copy), reads=[kv], writes=[k_("BDv")])
        if nsamp:
            sc = slice(nseq, Wt)
            sm = B["smp"]
            op("act", lambda e: e.activation(out=sm[:, 0, :], in_=sw[:, sc], func=AF.Exp, scale=-CW), reads=[t_("sw")], writes=[k_("smp")])
            op("dve", lambda e: e.tensor_scalar(out=sm[:, 1, :], in0=kk[:, sc], scalar1=-1.0, scalar2=None, op0=ALU.mult), reads=[t_("kk")], writes=[k_("smp")])
            op("dve", lambda e: e.tensor_copy(out=sm[:, 2, :], in_=a[:, sc]), reads=[t_("a")], writes=[k_("smp")])
            op("dve", lambda e: e.tensor_copy(out=sm[:, 3, :], in_=kh[:, sc]), reads=[t_("kh")], writes=[k_("smp")])

    ROT = [3, 5, 6]
    ROTA = [3, 5, 6, 0, 1]

    def wkv_phaseA(units):
        k_ = lambda pi_, nm: f"p{pi_}_{nm}"
        mall = cst[:, C_MALL:C_MALL + 512]
        for (c, pi_, ch, u) in units:
            B = PB[pi_]; bk = ROTA[u % 5]; pt = PS[bk]
            ARc = B["AR"][:, ch, :]; BDa = B["AR"][:, ch, 64:192]; rtn = B["AR"][:, ch, 0:64]
            BDb = B["BDb"][:, ch, :]; BDk = B["BDk"][:, ch, :]
            op("pe", lambda e, pt=pt, BDk=BDk, ARc=ARc: e.matmul(pt[:, 0:192], BDk, ARc, start=True, stop=True), reads=[k_(pi_, "BDk"), k_(pi_, "AR")], writes=[f"ps{bk}"])
            op("pe", lambda e, pt=pt, BDb=BDb, rtn=rtn: e.matmul(pt[:, 192:256], BDb, rtn, start=True, stop=True), reads=[k_(pi_, "BDb"), k_(pi_, "AR")], writes=[f"ps{bk}"])
            op("pe", lambda e, pt=pt, BDa=BDa, BDb=BDb: e.matmul(pt[:, 256:384], BDa, BDb, start=True, stop=True), reads=[k_(pi_, "BDb"), k_(pi_, "AR")], writes=[f"ps{bk}"])
            op("pe", lambda e, pt=pt, BDa=BDa, BDb=BDb: e.matmul(pt[:, 384:512], BDb, BDa, start=True, stop=True), reads=[k_(pi_, "BDb"), k_(pi_, "AR")], writes=[f"ps{bk}"])
            op("dve", lambda e, pt=pt, u=u: e.tensor_tensor(out=UB[:, u, 0:512], in0=pt[:, 0:512], in1=mall, op=ALU.mult), reads=[f"ps{bk}", "cst"], writes=[f"ub{u}"])
        TB = [(PS[4][:, 0:384], "ps4"), (PS[2][:, 0:192].bitcast(BF16), "ps2"), (PS[7][:, 0:192].bitcast(BF16), "ps7")]
        for (c, pi_, ch, u) in units:
            B = PB[pi_]
            tb, tk = TB[u % 3]
            for i, nm in enumerate(["BDb", "BDk", "BDv"]):
                op("pe", lambda e, i=i, tb=tb, src=B[nm][:, ch, :]: e.transpose(tb[:, 128 * i:128 * i + 128], src, identb[:]), reads=[k_(pi_, nm), "identb"], writes=[tk])
            op("act", lambda e, u=u, tb=tb: e.activation(out=UB[:, u, 640:1024], in_=tb, func=AF.Copy), reads=[tk], writes=[f"ub{u}"])
            op("dve", lambda e, u=u: e.tensor_copy(out=UB[:, u, 512:640], in_=identb[:]), reads=["identb"], writes=[f"ub{u}"])

        def level(u, bk):
            pt = PS[bk]
            N_ = UB[:, u, 256:384]; Nt_ = UB[:, u, 384:512]; Q_ = UB[:, u, 512:640]
            op("pe", lambda e: e.matmul(pt[:, 0:128], Nt_, N_, start=True, stop=True), reads=[f"ub{u}"], writes=[f"ps{bk}"])
            op("pe", lambda e: e.matmul(pt[:, 128:256], N_, Nt_, start=True, stop=True), reads=[f"ub{u}"], writes=[f"ps{bk}"])
            op("pe", lambda e: e.matmul(pt[:, 256:384], identb[:], Q_, start=True, stop=False), reads=[f"ub{u}", "identb"], writes=[f"ps{bk}"])
            op("pe", lambda e: e.matmul(pt[:, 256:384], N_, Q_, start=False, stop=True), reads=[f"ub{u}"], writes=[f"ps{bk}"])
            if u % 2 == 0:
                op("act", lambda e: e.activation(out=UB[:, u, 256:640], in_=pt[:, 0:384], func=AF.Copy), reads=[f"ps{bk}"], writes=[f"ub{u}"])
            else:
                op("dve", lambda e: e.tensor_copy(out=UB[:, u, 256:640], in_=pt[:, 0:384]), reads=[f"ps{bk}"], writes=[f"ub{u}"])

        for lev in range(1, 7):
            for (c, pi_, ch, u) in units:
                level(u, ROTA[u % 5])

    def wkv_phaseB(l, ulist):
        k_ = lambda pi_, nm: f"p{pi_}_{nm}"
        info = []
        for (c, pi_, ch, u) in ulist:
            B = PB[pi_]; bk = ROT[pi_ % 3]
            info.append(dict(c=c, pi_=pi_, ch=ch, u=u, B=B, bk=bk, pt=PS[bk], sbf=Sb[:, l, c, :], sf=Sf[:, l, c, :], skb=f"Sb{l}_{c}", skf=f"Sf{l}_{c}",
                             BDa=B["AR"][:, ch, 64:192], rtn=B["AR"][:, ch, 0:64], X=B["U"][:, 0, :], Ubd=B["U"][:, 1, :]))
        for d in info:
            u = d["u"]
            op("pe", lambda e, d=d: e.matmul(d["pt"][:, 0:128], d["BDa"], d["sbf"], start=True, stop=False), reads=[k_(d["pi_"], "AR"), d["skb"], "Sb"], writes=[f"ps{d['bk']}"])
            op("pe", lambda e, d=d, u=u: e.matmul(d["pt"][:, 0:128], UB[:, u, 64:192], UB[:, u, 896:1024], start=False, stop=True), reads=[f"ub{u}"], writes=[f"ps{d['bk']}"])
        for d in info:
            op("act", lambda e, d=d: e.activation(out=d["X"], in_=d["pt"][:, 0:128], func=AF.Copy), reads=[f"ps{d['bk']}"], writes=[k_(d["pi_"], "X")])
        for d in info:
            u = d["u"]
            op("pe", lambda e, d=d, u=u: e.matmul(d["pt"][:, 128:256], UB[:, u, 512:640], d["X"], start=True, stop=True), reads=[f"ub{u}", k_(d["pi_"], "X")], writes=[f"ps{d['bk']}"])
        for d in info:
            op("dve", lambda e, d=d: e.tensor_copy(out=d["Ubd"], in_=d["pt"][:, 128:256]), reads=[f"ps{d['bk']}"], writes=[k_(d["pi_"], "Ub")])
        for d in info:
            u = d["u"]; pi_ = d["pi_"]
            op("pe", lambda e, d=d: e.matmul(d["pt"][:, 256:320], d["sbf"], d["rtn"], start=True, stop=False), reads=[d["skb"], "Sb", k_(pi_, "AR")], writes=[f"ps{d['bk']}"])
            op("pe", lambda e, d=d, u=u: e.matmul(d["pt"][:, 256:320], d["Ubd"], UB[:, u, 192:256], start=False, stop=False), reads=[k_(pi_, "Ub"), f"ub{u}"], writes=[f"ps{d['bk']}"])
            op("pe", lambda e, d=d, u=u: e.matmul(d["pt"][:, 256:320], UB[:, u, 896:1024], UB[:, u, 0:64], start=False, stop=True), reads=[f"ub{u}"], writes=[f"ps{d['bk']}"])
        for d in info:
            pi_ = d["pi_"]; ch = d["ch"]; B = d["B"]
            op("act", lambda e, d=d, B=B, ch=ch: e.activation(out=B["yT"][:, 64 * ch:64 * ch + 64], in_=d["pt"][:, 256:320], func=AF.Copy), reads=[f"ps{d['bk']}"], writes=[k_(pi_, "yT")])
        for d in info:
            u = d["u"]; pi_ = d["pi_"]
            op("pe", lambda e, d=d, u=u: e.matmul(d["pt"][:, 320:448], UB[:, u, 640:768], d["Ubd"], start=True, stop=False), reads=[f"ub{u}", k_(pi_, "Ub")], writes=[f"ps{d['bk']}"])
            op("pe", lambda e, d=d, u=u: e.matmul(d["pt"][:, 320:448], UB[:, u, 768:896], UB[:, u, 896:1024], start=False, stop=True), reads=[f"ub{u}"], writes=[f"ps{d['bk']}"])
        for d in info:
            pi_ = d["pi_"]; ch = d["ch"]; B = d["B"]
            pc = B["E1"][:, 64 * ch + 63:64 * ch + 64]
            op("dve", lambda e, d=d, pc=pc: e.tensor_scalar(out=d["sf"], in0=d["sf"], scalar1=pc, scalar2=None, op0=ALU.mult), reads=[d["skf"], "Sf", k_(pi_, "E1")], writes=[d["skf"]])
            op("dve", lambda e, d=d, pc=pc: e.scalar_tensor_tensor(out=d["sf"], in0=d["pt"][:, 320:448], scalar=pc, in1=d["sf"], op0=ALU.mult, op1=ALU.add), reads=[f"ps{d['bk']}", d["skf"], k_(pi_, "E1")], writes=[d["skf"]])
            op("act", lambda e, d=d: e.activation(out=d["sbf"], in_=d["sf"], func=AF.Copy), reads=[d["skf"]], writes=[d["skb"]])

    BKS = [3, 5, 6, 0, 1]

    def wkv_sample(l, c, pi_, xi, nseq):
        B = PB[pi_]
        k_ = lambda nm: f"p{pi_}_{nm}"
        sc0 = nseq
        sm = B["smp"]
        kr, kv = f"xs{xi}", f"xs{8 + xi}"
        SK = ["sst0", "sst1", "dxp0", "dxp1", "st1", "st20", "st21"]
        op("dve", lambda e: e.memset(fence_t[:, 0:1], 0.0), writes=SK + UBK)
        H = 8

        def half(hf):
            bs = slice(H * hf, H * hf + H)
            scs = slice(sc0 + H * hf, sc0 + H * hf + H)
            sst_h = SBUFS["sst"][hf]; st2_h = SBUFS["st2"][hf]; st1 = SBUFS["st1"]
            qs = [(sm[:, 1, bs], [k_("smp")]), (sm[:, 0, bs], [k_("smp")]), (sm[:, 2, bs], [k_("smp")]), (sm[:, 3, bs], [k_("smp")]), (xs[:, xi, scs], [kr])]
            for qi, (q_ap, qk) in enumerate(qs):
                dx = SBUFS["dxp"][qi % 2]
                op("dve", lambda e, dx=dx, q_ap=q_ap: e.tensor_tensor(out=dx, in0=q_ap.unsqueeze(2).to_broadcast([128, H, 64]), in1=i64s.unsqueeze(1).to_broadcast([128, H, 64]), op=ALU.mult),
                   reads=qk + ["cst"], writes=[f"dxp{qi % 2}"])
                bk = BKS[qi]
                op("pe", lambda e, dx=dx, bk=bk: e.matmul(PS[bk][:, :], obd_f, dx.rearrange("p b k -> p (b k)"), start=True, stop=True), reads=[f"dxp{qi % 2}", "cst"], writes=[f"ps{bk}"])
            bcv = lambda qi: PS[BKS[qi]][:, :].rearrange("p (b k) -> p b k", k=64)
            bck = lambda qi: f"ps{BKS[qi]}"
            op("dve", lambda e: e.tensor_tensor(out=st1, in0=bcv(0), in1=sst_h, op=ALU.mult), reads=[bck(0), f"sst{hf}"], writes=["st1"])
            op("dve", lambda e: e.tensor_reduce(out=sa_t[:, bs], in_=st1, axis=AX.X, op=ALU.add), reads=["st1"], writes=["sa_t"])
            op("dve", lambda e: e.tensor_tensor(out=st2_h, in0=bcv(1), in1=sst_h, op=ALU.mult), reads=[bck(1), f"sst{hf}"], writes=[f"st2{hf}"])
            op("dve", lambda e: e.tensor_tensor(out=st1, in0=bcv(2), in1=sa_t[:, bs].unsqueeze(2).to_broadcast([128, H, 64]), op=ALU.mult), reads=[bck(2), "sa_t"], writes=["st1"])
            op("dve", lambda e: e.tensor_tensor(out=st2_h, in0=st2_h, in1=st1, op=ALU.add), reads=["st1", f"st2{hf}"], writes=[f"st2{hf}"])
            op("dve", lambda e: e.tensor_tensor(out=st1, in0=bcv(3), in1=xs[:, 8 + xi, scs].unsqueeze(2).to_broadcast([128, H, 64]), op=ALU.mult), reads=[bck(3), kv], writes=["st1"])
            op("dve", lambda e: e.tensor_tensor(out=st2_h, in0=st2_h, in1=st1, op=ALU.add), reads=["st1", f"st2{hf}"], writes=[f"st2{hf}"])
            P.dma("sp", owkv_s[l, bs, 2 * c:2 * c + 2].rearrange("b h v k -> (h v) b k"), st2_h, reads=[f"st2{hf}"])
            op("dve", lambda e: e.tensor_tensor(out=st1, in0=bcv(4), in1=st2_h, op=ALU.mult), reads=[bck(4), f"st2{hf}"], writes=["st1"])
            op("dve", lambda e: e.tensor_reduce(out=ys_t[:, bs], in_=st1, axis=AX.X, op=ALU.add), reads=["st1"], writes=["ys_t"])

        for hf in range(2):
            P.dma("sp", SBUFS["sst"][hf], swkv[l, H * hf:H * hf + H, 2 * c:2 * c + 2].rearrange("b h v k -> (h v) b k"), writes=[f"sst{hf}"])
        half(0)
        half(1)
        op("dve", lambda e: e.tensor_copy(out=B["yT"][:, sc0:sc0 + NSAMP], in_=ys_t[:]), reads=["ys_t"], writes=[k_("yT")])
        op("dve", lambda e: e.memset(fence_t[:, 1:2], 0.0), writes=SK + UBK)

    def wkv_post(l, c, pi_, xi, Wt):
        B = PB[pi_]
        PSs = PS[2] if pi_ == 0 else PS[7]
        psk = "ps2" if pi_ == 0 else "ps7"
        PT = PTS[pi_]
        k_ = lambda nm: f"p{pi_}_{nm}"
        yTf = B["yT"][:, 0:Wt]
        op("pe", lambda e: e.matmul(PSs[:, 0:Wt], obd64_r, B["yT"][:, 0:Wt], start=True, stop=True), reads=[k_("yT"), "cst"], writes=[psk])
        op("dve", lambda e: e.tensor_tensor(out=PT["kk"][:, 0:Wt], in0=yTf, in1=PSs[:, 0:Wt], op=ALU.subtract), reads=[psk, k_("yT")], writes=[f"pt{pi_}_kk"])
        op("act", lambda e: e.activation(out=B["f0"][:, 0:Wt], in_=PT["kk"][:, 0:Wt], func=AF.Square), reads=[f"pt{pi_}_kk"], writes=[k_("f0")])
        op("pe", lambda e: e.matmul(PSs[:, 0:Wt], obd64_r, B["f0"][:, 0:Wt], start=True, stop=True), reads=[k_("f0"), "cst"], writes=[psk])
        op("act", lambda e: e.activation(out=PT["kh"][:, 0:Wt], in_=PSs[:, 0:Wt], func=AF.Sqrt, bias=eps_t[:, 1:2]), reads=[psk, "eps"], writes=[f"pt{pi_}_kh"])
        op("dve", lambda e: e.reciprocal(out=PT["kh"][:, 0:Wt], in_=PT["kh"][:, 0:Wt]), reads=[f"pt{pi_}_kh"], writes=[f"pt{pi_}_kh"])
        op("dve", lambda e: e.tensor_tensor(out=PT["kk"][:, 0:Wt], in0=PT["kk"][:, 0:Wt], in1=PT["kh"][:, 0:Wt], op=ALU.mult), reads=[f"pt{pi_}_kk", f"pt{pi_}_kh"], writes=[f"pt{pi_}_kk"])
        op("dve", lambda e: e.tensor_scalar(out=PT["kk"][:, 0:Wt], in0=PT["kk"][:, 0:Wt], scalar1=pvc(l, O_LNW, c), scalar2=pvc(l, O_LNB, c), op0=ALU.mult, op1=ALU.add),
           reads=[f"pt{pi_}_kk", "pv"], writes=[f"pt{pi_}_kk"])
        op("dve", lambda e: e.tensor_tensor(out=PT["kk"][:, 0:Wt], in0=PT["kk"][:, 0:Wt], in1=B["bonus"][:, 0:Wt], op=ALU.add), reads=[f"pt{pi_}_kk", k_("bonus")], writes=[f"pt{pi_}_kk"])
        op("dve", lambda e: e.tensor_tensor(out=PT["kk"][:, 0:Wt], in0=PT["kk"][:, 0:Wt], in1=B["g"][:, 0:Wt], op=ALU.mult), reads=[f"pt{pi_}_kk", k_("g")], writes=[f"pt{pi_}_kk"])
        op("dve", lambda e: e.tensor_tensor(out=PT["kk"][:, 0:Wt], in0=PT["kk"][:, 0:Wt], in1=sga[xi][:, 0:Wt], op=ALU.mult), reads=[f"pt{pi_}_kk", f"sga{xi}"], writes=[f"pt{pi_}_kk"])
        op("dve", lambda e: e.tensor_tensor(out=big[:, c, 0:Wt], in0=PT["kk"][:, 0:Wt], in1=ybuf[xi][:, 0:Wt], op=ALU.add), reads=[f"pt{pi_}_kk", f"ybuf{xi}"], writes=[f"big{c}"])

    def mixer(l, ti, nseq, nsamp):
        Wt = nseq + nsamp
        nch = nseq // 64
        last = (ti == len(TILES) - 1)
        xnk = [f"xn{k}" for k in range(8)]
        rhs = lambda k: xn[:, k, 0:Wt]
        s, wv = wload(wi[l, :, 3072:3328], "in")
        b, pp = proj(s, wv, 0, 64, rhs, Wt, 8, xnk)
        shiftmix(l, 24, b, pp, nseq, nsamp, lambda: tmpA[0:64, 0:Wt], "tmpA", last)
        op("act", lambda e: e.activation(out=tlw[:, 0:Wt], in_=tmpA[0:64, 0:Wt], func=AF.Tanh), reads=["tmpA"], writes=["tlw"])
        b, pp = proj(s, wv, 64, 64, rhs, Wt, 8, xnk)
        shiftmix(l, 25, b, pp, nseq, nsamp, lambda: lat[:, 0:Wt], "lat", last)
        b, pp = proj(s, wv, 128, 128, rhs, Wt, 8, xnk)
        shiftmix(l, 26, b, pp, nseq, nsamp, lambda: tmpA[:, 0:Wt], "tmpA", last)
        op("act", lambda e: e.activation(out=slg[:, 0:Wt], in_=tmpA[:, 0:Wt], func=AF.Sigmoid), reads=["tmpA"], writes=["slg"])
        for q in range(2):
            for ty in range(3):
                s, wv = wload(wi[l, :, 1024 * ty + 512 * q:1024 * ty + 512 * q + 512], "in")
                for pp_ in range(4):
                    b, pp = proj(s, wv, 128 * pp_, 128, rhs, Wt, 8, xnk)
                    bi = 4 * ty + pp_
                    shiftmix(l, 8 * ty + 4 * q + pp_, b, pp, nseq, nsamp, lambda bi=bi: xs[:, bi, 0:Wt], f"xs{bi}", last)
            rec_lists = []
            for xi, pi_ in [(i, i) for i in range(NPF)]:
                P.begin()
                wkv_prep(l, 4 * q + xi, pi_, xi, nseq, nsamp)
                rec_lists.append(P.end())
            P.begin()
            CB = DSH
            s, wv = wload(wi[l, :, CB + 2048 + 512 * q:CB + 2048 + 512 * q + 512], "in")
            s2, wv2 = wload(wi[l, :, CB + 1024 + 512 * q:CB + 1024 + 512 * q + 512], "in")
            for pp_ in range(4):
                c = 4 * q + pp_
                b, pp = proj(s, wv, 128 * pp_, 128, rhs, Wt, 8, xnk)
                op("act", lambda e, pp=pp: e.activation(out=tmpA[:, 0:Wt], in_=pp, func=AF.Copy), reads=[f"ps{b}"], writes=["tmpA"])
                b2, pp2 = proj(s2, wv2, 128 * pp_, 128, rhs, Wt, 8, xnk)
                ub = ubuf[pp_]
                op("act", lambda e, ub=ub, c=c: e.activation(out=ub[:, 0:2], in_=ucarry[:, l, c, :], func=AF.Copy), reads=[f"ucarry{l}_{c}", "ucarry"], writes=[f"ubuf{pp_}"])
                op("dve", lambda e, ub=ub, pp2=pp2: e.tensor_tensor(out=ub[:, 2:2 + Wt], in0=pp2, in1=tmpA[:, 0:Wt], op=ALU.mult), reads=[f"ps{b2}", "tmpA"], writes=[f"ubuf{pp_}"])
                op("act", lambda e, ub=ub, c=c: e.activation(out=ucarry[:, l, c, :], in_=ub[:, nseq:nseq + 2], func=AF.Copy), reads=[f"ubuf{pp_}"], writes=[f"ucarry{l}_{c}"])
                if last:
                    P.dma("sp", oconv_p[l, :, c * 128:(c + 1) * 128].rearrange("j f -> f j"), ucarry[:, l, c, :], reads=[f"ucarry{l}_{c}"], slow=True, key="out")
                yb = ybuf[pp_]
                cw = lambda j, c=c: pvc(l, O_CW + 8 * j, c)
                op("dve", lambda e, ub=ub, yb=yb, cw=cw: e.tensor_scalar(out=yb[:, 0:nseq], in0=ub[:, 0:nseq], scalar1=cw(0), scalar2=None, op0=ALU.mult), reads=[f"ubuf{pp_}", "pv"], writes=[f"ybuf{pp_}"])
                op("dve", lambda e, ub=ub, yb=yb, cw=cw: e.scalar_tensor_tensor(out=yb[:, 0:nseq], in0=ub[:, 1:nseq + 1], scalar=cw(1), in1=yb[:, 0:nseq], op0=ALU.mult, op1=ALU.add), reads=[f"ubuf{pp_}", "pv", f"ybuf{pp_}"], writes=[f"ybuf{pp_}"])
                op("dve", lambda e, ub=ub, yb=yb, cw=cw: e.scalar_tensor_tensor(out=yb[:, 0:nseq], in0=ub[:, 2:nseq + 2], scalar=cw(2), in1=yb[:, 0:nseq], op0=ALU.mult, op1=ALU.add), reads=[f"ubuf{pp_}", "pv", f"ybuf{pp_}"], writes=[f"ybuf{pp_}"])
                if nsamp:
                    sc = slice(nseq, Wt)
                    usl = ub[:, 2 + nseq:2 + Wt]
                    op("dve", lambda e, yb=yb, cw=cw, c=c: e.tensor_scalar(out=yb[:, sc], in0=scvT[:, l, c, 0, :], scalar1=cw(0), scalar2=None, op0=ALU.mult), reads=[f"scvT{l}_{c}_0", "pv"], writes=[f"ybuf{pp_}"])
                    op("dve", lambda e, yb=yb, cw=cw, c=c: e.scalar_tensor_tensor(out=yb[:, sc], in0=scvT[:, l, c, 1, :], scalar=cw(1), in1=yb[:, sc], op0=ALU.mult, op1=ALU.add), reads=[f"scvT{l}_{c}_1", "pv", f"ybuf{pp_}"], writes=[f"ybuf{pp_}"])
                    op("dve", lambda e, yb=yb, cw=cw, usl=usl: e.scalar_tensor_tensor(out=yb[:, sc], in0=usl, scalar=cw(2), in1=yb[:, sc], op0=ALU.mult, op1=ALU.add), reads=[f"ubuf{pp_}", "pv", f"ybuf{pp_}"], writes=[f"ybuf{pp_}"])
                    op("act", lambda e, usl=usl, c=c: e.activation(out=ocvT[:, l, c, :], in_=usl, func=AF.Copy), reads=[f"ubuf{pp_}"], writes=[f"ocvT{l}_{c}"])
                    P.dma("sp", oconv_s[l, :, 0, c * 128:(c + 1) * 128].rearrange("b f -> f b"), scvT[:, l, c, 1, :], reads=[f"scvT{l}_{c}_1"], slow=True, key="out")
                    P.dma("sp", oconv_s[l, :, 1, c * 128:(c + 1) * 128].rearrange("b f -> f b"), ocvT[:, l, c, :], reads=[f"ocvT{l}_{c}"], slow=True, key="out")
            s, wv = wload(wi[l, :, CB + 512 * q:CB + 512 * q + 512], "in")
            for pp_ in range(4):
                b, pp = proj(s, wv, 128 * pp_, 128, rhs, Wt, 8, xnk)
                op("dve", lambda e, pp=pp, pp_=pp_: e.tensor_tensor(out=ybuf[pp_][:, 0:Wt], in0=pp, in1=ybuf[pp_][:, 0:Wt], op=ALU.mult), reads=[f"ps{b}", f"ybuf{pp_}"], writes=[f"ybuf{pp_}"])
            s, wv = wload(wi[l, :, CB + 3072 + 1024 + 512 * q:CB + 3072 + 1024 + 512 * q + 512], "in")
            for pp_ in range(4):
                b, pp = proj(s, wv, 128 * pp_, 128, rhs, Wt, 8, xnk)
                op("act", lambda e, pp=pp: e.activation(out=tmpA[:, 0:Wt], in_=pp, func=AF.Sigmoid), reads=[f"ps{b}"], writes=["tmpA"])
                op("dve", lambda e, pp_=pp_: e.tensor_tensor(out=ybuf[pp_][:, 0:Wt], in0=tmpA[:, 0:Wt], in1=ybuf[pp_][:, 0:Wt], op=ALU.mult), reads=["tmpA", f"ybuf{pp_}"], writes=[f"ybuf{pp_}"])
            s, wv = wload(wi[l, :, CB + 3072 + 512 * q:CB + 3072 + 512 * q + 512], "in")
            for pp_ in range(4):
                b, pp = proj(s, wv, 128 * pp_, 128, rhs, Wt, 8, xnk)
                op("act", lambda e, pp=pp, pp_=pp_: e.activation(out=sga[pp_][:, 0:Wt], in_=pp, func=AF.Sigmoid), reads=[f"ps{b}"], writes=[f"sga{pp_}"])
            rec_lists.append(P.end())
            P.play(rec_lists)
            for sub in range(4 // NPF):
                prs = [(NPF * sub + i, i) for i in range(NPF)]
                if sub > 0:
                    rl = []
                    for xi, pi_ in prs:
                        P.begin()
                        wkv_prep(l, 4 * q + xi, pi_, xi, nseq, nsamp)
                        rl.append(P.end())
                    P.play(rl)
                units = [(4 * q + xi, pi_, ch, pi_ * NCH + ch) for xi, pi_ in prs for ch in range(nch)]
                wkv_phaseA(units)
                for ch in range(nch):
                    wkv_phaseB(l, [un for un in units if un[2] == ch])
                for xi, pi_ in prs:
                    c = 4 * q + xi
                    if nsamp:
                        wkv_sample(l, c, pi_, xi, nseq)
                    if last:
                        op("pe", lambda e, c=c: e.transpose(PS[7][:, 0:128], Sf[:, l, c, :], ident), reads=[f"Sf{l}_{c}", "Sf", "cst"], writes=["ps7"])
                        op("act", lambda e: e.activation(out=fin[:], in_=PS[7][:, 0:128], func=AF.Copy), reads=["ps7"], writes=["fin"])
                        for h in range(2):
                            P.dma("sp", owkv_p[l, 2 * c + h], fin[64 * h:64 * h + 64, 64 * h:64 * h + 64], reads=["fin"])
                rl = []
                for xi, pi_ in prs:
                    P.begin()
                    wkv_post(l, 4 * q + xi, pi_, xi, Wt)
                    rl.append(P.end())
                P.play(rl)
        for hh in range(2):
            s, wv = wload(wo[l, :, 512 * hh:512 * hh + 512], "in")
            for pp_ in range(4):
                oc = 4 * hh + pp_
                b, pp = proj(s, wv, 128 * pp_, 128, lambda k: big[:, k, 0:Wt], Wt, 8, [f"big{k}" for k in range(8)])
                op("dve", lambda e, pp=pp, oc=oc: e.tensor_tensor(out=xT[:, oc, 0:Wt], in0=pp, in1=xT[:, oc, 0:Wt], op=ALU.add), reads=[f"ps{b}", f"xT{oc}"], writes=[f"xT{oc}"])
        if ti == 0:
            for oc in range(8):
                op("dve", lambda e, oc=oc: e.memset(xT[:, oc, 0:48], 0.0), writes=[f"xT{oc}"])

    def load_rows(blk_rows):
        for (r0, nr, src) in blk_rows:
            if src is None:
                op("dve", lambda e, r0=r0, nr=nr: e.memset(xtok[r0:r0 + nr, :], 0.0), writes=["xtok"])
            else:
                P.dma("sp", xtok[r0:r0 + nr, :], src, writes=["xtok"])

    for ti, (Wt, nch, nsamp) in enumerate(TILES):
        wstate["j"] = 0
        wstate["tile"] = ti
        nseq = nch * 64
        nblk = (Wt + 127) // 128
        for j in range(nblk):
            nr = min(128, Wt - 128 * j)
            if ti == 0 and j == 0:
                rows = [(0, 64, None), (48, 16, meta[:, :]), (64, 64, xp[0:64, :])]
                rows = [(0, 32, None), (32, 32, None), (48, 16, meta[:, :]), (64, 64, xp[0:64, :])]
            else:
                i0 = TW * ti + 128 * j - 64
                ns = max(0, min(128, nseq - 128 * j))
                rows = []
                if ns:
                    rows.append((0, ns, xp[i0:i0 + ns, :]))
                if nsamp and ns < nr:
                    rows.append((ns, NSAMP, xsm[:, :]))
            load_rows(rows)
            for half in range(2):
                for cc in range(4):
                    c = 4 * half + cc
                    op("pe", lambda e, c=c, cc=cc, nr=nr: e.transpose(PS[7][:, 128 * cc:128 * cc + nr], xtok[0:nr, c * 128:(c + 1) * 128], ident[0:nr, 0:nr]),
                       reads=["xtok", "cst"], writes=["ps7"])
                for cc in range(4):
                    c = 4 * half + cc
                    op("act", lambda e, c=c, cc=cc, nr=nr, j=j: e.activation(out=xT[:, c, 128 * j:128 * j + nr], in_=PS[7][:, 128 * cc:128 * cc + nr], func=AF.Copy),
                       reads=["ps7"], writes=[f"xT{c}"])
        for l in range(NL):
            rmsnorm(lambda c, l=l: pvc(l, O_NF1, c), Wt, "f1")
            ffn(l, f1i, f1o, Wt)
            rmsnorm(lambda c, l=l: pvc(l, O_NMIX, c), Wt, "mix")
            mixer(l, ti, nseq, nsamp)
            rmsnorm(lambda c, l=l: pvc(l, O_NF2, c), Wt, "f2")
            ffn(l, f2i, f2o, Wt)
        rmsnorm(lambda c: pv[:, 2 * PVL + c:2 * PVL + c + 1], Wt, "fin", final=True)
        for j in range(nblk):
            nr = min(128, Wt - 128 * j)
            for half in range(2):
                for cc in range(4):
                    c = 4 * half + cc
                    op("pe", lambda e, c=c, cc=cc, nr=nr, j=j: e.transpose(PS[7][0:nr, 128 * cc:128 * cc + 128], xT[:, c, 128 * j:128 * j + nr], ident),
                       reads=[f"xT{c}", "cst"], writes=["ps7"])
                op("act", lambda e, half=half, nr=nr: e.activation(out=ytok[0:nr, 512 * half:512 * half + 512], in_=PS[7][0:nr, :], func=AF.Copy), reads=["ps7"], writes=["xtok"])
            if ti == 0 and j == 0:
                P.dma("sp", yp[0:64, :], ytok[64:128, :], reads=["xtok"])
            else:
                i0 = TW * ti + 128 * j - 64
                ns = max(0, min(128, nseq - 128 * j))
                if ns:
                    P.dma("sp", yp[i0:i0 + ns, :], ytok[0:ns, :], reads=["xtok"])
                if nsamp and ns < nr:
                    P.dma("sp", ys[:, :], ytok[ns:ns + NSAMP, :], reads=["xtok"])
    P.emit()
    es.close()
    return nc


def _fm(v):
    return np.ascontiguousarray(np.asarray(v, np.float32).reshape(8, 128).T)


def _consts():
    c = np.zeros((128, NCST), np.float32)
    p = np.arange(128)
    c[:, C_ID:C_ID + 128] = np.eye(128, dtype=np.float32)
    c[:, C_ON:C_ON + 128] = 1.0 / 1024.0
    same = (p[:, None] // 64) == (p[None, :] // 64)
    c[:, C_OBD:C_OBD + 128] = same
    c[:, C_OBD64:C_OBD64 + 128] = same / 64.0
    msu = same & ((p[None, :] % 64) > (p[:, None] % 64))
    msl = same & ((p[None, :] % 64) < (p[:, None] % 64))
    t = np.arange(64)
    miu = t[None, :] >= (p[:, None] % 64)
    c[:, C_MALL:C_MALL + 64] = miu
    c[:, C_MALL + 64:C_MALL + 192] = msu
    c[:, C_MALL + 192:C_MALL + 256] = miu
    c[:, C_MALL + 256:C_MALL + 384] = msl
    c[:, C_MALL + 384:C_MALL + 512] = msu
    c[:, C_I64:C_I64 + 64] = t[None, :] == (p[:, None] % 64)
    r = np.ones(320, np.float32)
    r[::64] = 0.0
    c[:, C_RST:C_RST + 320] = r[None, :]
    return c


_CACHE = {}


def make_maps(x_prompt, x_sample, state_wkv, state_shift, state_conv, meta_tokens,
              norm_ffn1, ffn1_w_in, ffn1_w_out, norm_mix, w_in, mu_shift, w0, w_w2, a0,
              w_a2, w_g2, k_k, k_a, r_k, lnx_w, lnx_b, conv_w, w_o, norm_ffn2,
              ffn2_w_in, ffn2_w_out, norm_final):
    f = lambda a: np.ascontiguousarray(np.asarray(a, dtype=np.float32))
    x_prompt = f(x_prompt); x_sample = f(x_sample); state_wkv = f(state_wkv); state_shift = f(state_shift); state_conv = f(state_conv)
    pv = np.zeros((128, NV), np.float32)
    for l in range(NL):
        o = l * PVL
        pv[:, o + O_NF1:o + O_NF1 + 8] = _fm(norm_ffn1[l]); pv[:, o + O_NMIX:o + O_NMIX + 8] = _fm(norm_mix[l]); pv[:, o + O_NF2:o + O_NF2 + 8] = _fm(norm_ffn2[l])
        mu = np.asarray(mu_shift[l], np.float32)
        for pi, (c0, R) in enumerate(PIECES):
            pv[0:R, o + O_MU + pi] = mu[c0:c0 + R]
        pv[:, o + O_W0:o + O_W0 + 8] = _fm(w0[l]); pv[:, o + O_A0:o + O_A0 + 8] = _fm(a0[l]); pv[:, o + O_KK:o + O_KK + 8] = _fm(k_k[l]); pv[:, o + O_KA:o + O_KA + 8] = _fm(k_a[l])
        pv[:, o + O_RK:o + O_RK + 8] = _fm(np.asarray(r_k[l]).reshape(-1)); pv[:, o + O_LNW:o + O_LNW + 8] = _fm(lnx_w[l]); pv[:, o + O_LNB:o + O_LNB + 8] = _fm(lnx_b[l])
        for j in range(3):
            pv[:, o + O_CW + 8 * j:o + O_CW + 8 * j + 8] = _fm(np.asarray(conv_w[l])[j])
    pv[:, 2 * PVL:2 * PVL + 8] = _fm(norm_final)
    cst = _consts()
    shared = {"meta": f(meta_tokens), "f1i": f(ffn1_w_in), "f1o": f(ffn1_w_out), "wi": f(w_in), "wo": f(w_o), "f2i": f(ffn2_w_in), "f2o": f(ffn2_w_out),
              "ww2": f(w_w2), "wa2": f(w_a2), "wg2": f(w_g2), "pv": pv, "cst": cst}
    in_maps = []
    for b in range(8):
        m = dict(shared)
        m["xp"] = x_prompt[b]
        m["xsm"] = np.ascontiguousarray(x_sample[16 * b:16 * b + 16, 0, :])
        m["swkv"] = np.ascontiguousarray(state_wkv[:, 16 * b:16 * b + 16])
        m["sshift"] = np.ascontiguousarray(state_shift[:, 16 * b:16 * b + 16])
        m["sconv"] = np.ascontiguousarray(state_conv[:, 16 * b:16 * b + 16])
        in_maps.append(m)
    return in_maps


def gather(R):
    n = len(R)
    y_prompt = np.stack([R[b]["yp"] for b in range(n)], 0)
    y_sample = np.concatenate([R[b]["ys"] for b in range(n)], 0)[:, None, :]
    p_wkv = np.stack([R[b]["owkv_p"] for b in range(n)], 1)
    p_shift = np.stack([R[b]["oshift_p"] for b in range(n)], 1)
    p_conv = np.stack([R[b]["oconv_p"] for b in range(n)], 1)
    s_wkv = np.concatenate([R[b]["owkv_s"] for b in range(n)], 1)
    s_shift = np.concatenate([R[b]["oshift_s"] for b in range(n)], 1)
    s_conv = np.concatenate([R[b]["oconv_s"] for b in range(n)], 1)
    return tuple(np.ascontiguousarray(a, dtype=np.float32) for a in (y_prompt, y_sample, p_wkv, p_shift, p_conv, s_wkv, s_shift, s_conv))


def kernel(**inputs):
    in_maps = make_maps(**inputs)
    if "nc" not in _CACHE:
        _CACHE["nc"] = build_program()
    res = run_bass_kernel_spmd(_CACHE["nc"], in_maps, core_ids=list(range(8)))
    return gather(res.results)
```

```python
import numpy as np
from contextlib import ExitStack
import concourse.bass as bass
import concourse.mybir as mybir
from concourse.bass_utils import run_bass_kernel_spmd

F32 = mybir.dt.float32
F32R = mybir.dt.float32r
BF16 = mybir.dt.bfloat16
AF = mybir.ActivationFunctionType
ALU = mybir.AluOpType
AX = mybir.AxisListType

D = 1024
DFF = 2816
DPROJ = 8448
DSH = 3328
NL = 2
NSAMP = 16
CW = 0.6065306597126334
TW = 320
SEQ = 2048
WMAX = TW
NSLOT = 6
NPF = 2
PVL = 131
NV = 2 * PVL + 8
O_NF1, O_NMIX, O_NF2, O_MU, O_W0, O_A0, O_KK, O_KA, O_RK, O_LNW, O_LNB, O_CW = 0, 8, 16, 24, 51, 59, 67, 75, 83, 91, 99, 107
C_ID, C_ON, C_OBD, C_OBD64, C_MALL, C_I64, C_RST = 0, 128, 256, 384, 512, 1024, 1088
NCST = 1408
PIECES = [(128 * i, 128) for i in range(24)] + [(3072, 64), (3136, 64), (3200, 128)]


class Prog:
    ENGS = ["pe", "act", "dve", "pool", "sp"]

    def __init__(self, nc):
        self.nc = nc
        self.streams = {e: [] for e in self.ENGS}
        self.count = {}
        self.res = {}
        self.seen = {e: {} for e in self.ENGS}
        self.semkeys = []
        self.ndma = 0

    def _tok(self, key, amt):
        if key not in self.count:
            self.count[key] = 0
            self.semkeys.append(key)
        self.count[key] += amt
        return (key, self.count[key])

    def begin(self):
        self.rec = []

    def end(self):
        r, self.rec = self.rec, None
        return r

    def play(self, lists):
        lists = [l for l in lists if l]
        pos = [0] * len(lists)
        total = sum(len(l) for l in lists)
        for step in range(total):
            best, bi = None, None
            for i, l in enumerate(lists):
                if pos[i] < len(l):
                    frac = pos[i] / len(l)
                    if best is None or frac < best:
                        best, bi = frac, i
            a = lists[bi][pos[bi]]
            pos[bi] += 1
            self.op(a[0], a[1], reads=a[2], writes=a[3], dma=a[4])

    def op(self, eng, fn, reads=(), writes=(), dma=None):
        if getattr(self, "rec", None) is not None:
            self.rec.append((eng, fn, tuple(reads), tuple(writes), dma))
            return None
        waits = {}

        def need(tok):
            if tok is None:
                return
            k, v = tok
            if eng == "pe" and k == "pe":
                return
            if self.seen[eng].get(k, 0) >= v:
                return
            waits[k] = max(waits.get(k, 0), v)

        for r in reads:
            st = self.res.get(r)
            if st:
                need(st["w"])
        for w in writes:
            st = self.res.get(w)
            if st:
                need(st["w"])
                for t in st["r"]:
                    need(t)
        for k, v in waits.items():
            self.seen[eng][k] = v
        tok = self._tok(dma, 16) if dma else self._tok(eng, 1)
        for r in reads:
            st = self.res.setdefault(r, {"w": None, "r": []})
            st["r"].append(tok)
        for w in writes:
            self.res[w] = {"w": tok, "r": []}
        self.streams[eng].append((fn, list(waits.items()), tok, 16 if dma else 1))
        return tok

    def dma(self, eng, out, in_, reads=(), writes=(), slow=False, key=None):
        key = "dma_" + (key if key else str((writes or reads)[0]))
        if slow:
            f = lambda e, o=out, i=in_: e.dma_start(out=o, in_=i, allow_slow_non_contiguous=True)
        else:
            f = lambda e, o=out, i=in_: e.dma_start(out=o, in_=i)
        return self.op(eng, f, reads=reads, writes=writes, dma=key)

    def seal(self, key, resources):
        tok = ("dma_" + key, self.count["dma_" + key])
        for r in resources:
            if r in self.res:
                self.res[r]["w"] = tok

    def emit(self):
        nc = self.nc
        with ExitStack() as es:
            sems = {k: es.enter_context(nc.semaphore(f"s_{i}")) for i, k in enumerate(self.semkeys)}
            block = es.enter_context(nc.Block())

            def mk(engname):
                def body(e):
                    for fn, waits, tok, amt in self.streams[engname]:
                        for k, v in waits:
                            e.wait_ge(sems[k], v)
                        inst = fn(e)
                        inst.then_inc(sems[tok[0]], amt)
                    if engname == "sp":
                        for k in self.semkeys:
                            e.wait_ge(sems[k], self.count[k])
                return body

            block.tensor(mk("pe"))
            block.scalar(mk("act"))
            block.vector(mk("dve"))
            block.gpsimd(mk("pool"))
            block.sync(mk("sp"))


def build_program(SEQ=SEQ, NL=NL):
    total = SEQ + 64
    nfull = (total - 1) // TW
    rem = total - nfull * TW
    assert rem % 64 == 0 and 64 <= rem and rem + NSAMP <= TW
    TILES = [(TW, TW // 64, 0)] * nfull + [(rem + NSAMP, rem // 64, NSAMP)]
    nc = bass.Bass("TRN2", target_bir_lowering=False)
    es = ExitStack()

    def din(name, shape):
        return nc.dram_tensor(name, list(shape), F32, kind="ExternalInput").ap()

    def dout(name, shape):
        return nc.dram_tensor(name, list(shape), F32, kind="ExternalOutput").ap()

    xp = din("xp", [SEQ, D]); xsm = din("xsm", [NSAMP, D]); meta = din("meta", [16, D])
    swkv = din("swkv", [NL, NSAMP, 16, 64, 64]); sshift = din("sshift", [NL, NSAMP, DSH]); sconv = din("sconv", [NL, NSAMP, 2, D])
    f1i = din("f1i", [NL, D, 2 * DFF]); f1o = din("f1o", [NL, DFF, D]); wi = din("wi", [NL, D, DPROJ]); wo = din("wo", [NL, D, D])
    f2i = din("f2i", [NL, D, 2 * DFF]); f2o = din("f2o", [NL, DFF, D])
    ww2 = din("ww2", [NL, 64, D]); wa2 = din("wa2", [NL, 64, D]); wg2 = din("wg2", [NL, 128, D])
    pvd = din("pv", [128, NV]); cstd = din("cst", [128, NCST])
    yp = dout("yp", [SEQ, D]); ys = dout("ys", [NSAMP, D])
    owkv_p = dout("owkv_p", [NL, 16, 64, 64]); oshift_p = dout("oshift_p", [NL, DSH]); oconv_p = dout("oconv_p", [NL, 2, D])
    owkv_s = dout("owkv_s", [NL, NSAMP, 16, 64, 64]); oshift_s = dout("oshift_s", [NL, NSAMP, DSH]); oconv_s = dout("oconv_s", [NL, NSAMP, 2, D])

    def sb(name, shape, dt=F32):
        return es.enter_context(nc.sbuf_tensor(name, list(shape), dt))

    W = WMAX
    xT = sb("xT", [128, 8, W]); xn = sb("xn", [128, 8, W], BF16)
    big = sb("big", [128, 22, W], BF16)
    xs = sb("xs", [128, 12, W], BF16)
    wslots = [sb(f"ws{i}", [128, 4096], BF16) for i in range(NSLOT)]
    pv = sb("pvt", [128, NV]); cst = sb("cstt", [128, NCST]); onesNb = sb("onesNb", [128, 128], BF16)
    omka = sb("omka", [128, NL, 8]); identb = sb("identb", [128, 128], BF16)
    xtok = sb("xtok", [128, D]); ytok = xtok
    rstd = sb("rstd", [128, W]); tmpA = sb("tmpA", [128, W]); tmpB = sb("tmpB", [128, W])
    psb = [sb(f"psb{i}", [128, W + 1]) for i in range(2)]
    carry = sb("carry", [128, NL, 27]); ucarry = sb("ucarry", [128, NL, 8, 2])
    sshT = sb("sshT", [128, NL, 27, NSAMP]); scvT = sb("scvT", [128, NL, 8, 2, NSAMP])
    oshT = sb("oshT", [128, NL, 27, NSAMP]); ocvT = sb("ocvT", [128, NL, 8, NSAMP])
    tlw = sb("tlw", [64, W], BF16); lat = sb("lat", [64, W], BF16); slg = sb("slg", [128, W], BF16)
    ubuf = [sb(f"ubuf{i}", [128, W + 2]) for i in range(4)]
    ybuf = [sb(f"ybuf{i}", [128, W], BF16) for i in range(4)]
    sga = [sb(f"sga{i}", [128, W], BF16) for i in range(4)]
    PTS = []
    for i in range(NPF):
        d = {nm: sb(f"pt{i}_{nm}", [128, W]) for nm in ["sw", "cs", "E2", "E3", "a", "kk", "kh", "t2"]}
        d["f0"] = sb(f"pt{i}_f0", [128, W])
        d["btn"] = sb(f"pt{i}_btn", [128, W], BF16)
        PTS.append(d)
    NCH = W // 64
    PB = []
    for i in range(NPF):
        d = {}
        for nm in ["E1", "bonus", "g"]:
            d[nm] = sb(f"p{i}_{nm}", [128, W])
        d["f0"] = sb(f"p{i}_f0", [128, W])
        d["yT"] = sb(f"p{i}_yT", [128, W])
        d["smp"] = sb(f"p{i}_smp", [128, 4, NSAMP])
        d["lora"] = sb(f"p{i}_lora", [128, 3, 128], BF16)
        d["AR"] = sb(f"p{i}_AR", [128, NCH, 192], BF16)
        d["BDb"] = sb(f"p{i}_BDb", [128, NCH, 128], BF16)
        d["BDk"] = sb(f"p{i}_BDk", [128, NCH, 128], BF16)
        d["BDv"] = sb(f"p{i}_BDv", [128, NCH, 128], BF16)
        d["U"] = sb(f"p{i}_U", [128, 2, 128], BF16)
        PB.append(d)
    NU = NPF * NCH
    arena = sb("arena", [128, NU * 512], F32)
    UB = arena[:].bitcast(BF16).rearrange("p (u c) -> p u c", c=1024)
    UBK = [f"ub{u}" for u in range(NU)]
    Sf = sb("Sf", [128, NL, 8, 128]); Sb = sb("Sbf", [128, NL, 8, 128], BF16)
    assert NU * 512 >= 7 * 512
    _ab = lambda i: arena[:, 512 * i:512 * (i + 1)].rearrange("p (b k) -> p b k", k=64)
    SBUFS = {"sst": [_ab(0), _ab(1)], "dxp": [_ab(2), _ab(3)], "st1": _ab(4), "st2": [_ab(5), _ab(6)]}
    fence_t = sb("fence_t", [128, 2]); sa_t = sb("sa_t", [128, NSAMP]); ys_t = sb("ys_t", [128, NSAMP]); fin = sb("fin", [128, 128])
    PS = [es.enter_context(nc.psum_tensor(f"ps{i}", [128, 512], F32)) for i in range(8) if i != 4]
    PS.insert(4, es.enter_context(nc.psum_tensor("ps4", [128, 1024], BF16)))
    asf = lambda ap: ap.bitcast(F32)

    P = Prog(nc)
    op = P.op

    P.dma("sp", pv[:], pvd[:, :], writes=["pv"])
    P.dma("sp", cst[:], cstd[:, :], writes=["cst"])
    op("dve", lambda e: e.tensor_copy(out=identb[:], in_=cst[:, C_ID:C_ID + 128]), reads=["cst"], writes=["identb"])
    op("dve", lambda e: e.tensor_copy(out=onesNb[:], in_=cst[:, C_ON:C_ON + 128]), reads=["cst"], writes=["onesNb"])
    for l in range(NL):
        op("dve", lambda e, l=l: e.tensor_scalar(out=omka[:, l, :], in0=pv[:, l * PVL + O_KA:l * PVL + O_KA + 8], scalar1=-1.0, scalar2=1.0, op0=ALU.mult, op1=ALU.add),
           reads=["pv"], writes=[f"omka{l}"])
    op("dve", lambda e: e.memset(carry[:], 0.0), writes=["carry"])
    op("dve", lambda e: e.memset(ucarry[:], 0.0), writes=["ucarry"])
    op("dve", lambda e: e.memset(Sf[:], 0.0), writes=["Sf"])
    op("dve", lambda e: e.memset(Sb[:], 0.0), writes=["Sb"])
    for i in range(NPF):
        for nm in ["AR", "BDb", "BDk", "BDv"]:
            op("dve", lambda e, t=PB[i][nm]: e.memset(t[:], 0.0), writes=[f"p{i}_{nm}"])
    setup_res = []
    for l in range(NL):
        for pi, (c0, R) in enumerate(PIECES):
            P.dma("sp", sshT[0:R, l, pi, :], sshift[l, :, c0:c0 + R].rearrange("b f -> f b"), writes=[f"sshT{l}_{pi}"], slow=True, key="setup")
            setup_res.append(f"sshT{l}_{pi}")
        for c in range(8):
            for j in range(2):
                P.dma("sp", scvT[:, l, c, j, :], sconv[l, :, j, c * 128:(c + 1) * 128].rearrange("b f -> f b"), writes=[f"scvT{l}_{c}_{j}"], slow=True, key="setup")
                setup_res.append(f"scvT{l}_{c}_{j}")
    P.seal("setup", setup_res)

    def pvc(l, off, c):
        return pv[:, l * PVL + off + c:l * PVL + off + c + 1]

    ident = cst[:, C_ID:C_ID + 128]
    obd_f = cst[:, C_OBD:C_OBD + 128]; obd_r = obd_f; obd64_r = cst[:, C_OBD64:C_OBD64 + 128]
    i64s = cst[:, C_I64:C_I64 + 64]

    wstate = {"n": 0, "j": 0, "tile": 0}
    NBLK = 59 * NL
    wc = nc.dram_tensor("wcache", [NBLK, 128, 4096], BF16, kind="Internal").ap()

    def wload(src_ap, shape_str, **kw):
        i = wstate["n"] % NSLOT
        wstate["n"] += 1
        j = wstate["j"]
        wstate["j"] += 1
        t = wslots[i]
        n = 1
        for v in src_ap.shape:
            n *= v
        per = n // 128
        kk_ = 8 if shape_str == "in" else 22
        view = t[:, 0:per].rearrange("p (k n) -> p k n", k=kk_)
        if wstate["tile"] == 0:
            P.dma("pool", view, src_ap.rearrange("(k p) n -> p k n", p=128), writes=[f"ws{i}"])
            P.dma("sp", wc[j, :, 0:per], t[:, 0:per], reads=[f"ws{i}"], writes=[f"wc{j}"], key=f"wst{i}")
        else:
            P.dma("pool", t[:, 0:per], wc[j, :, 0:per], reads=[f"wc{j}"], writes=[f"ws{i}"])
        return (i, view)

    mmtoggle = {"n": 0}

    MMB = [0, 1, 3, 5, 6]

    def mmbank():
        mmtoggle["n"] = (mmtoggle["n"] + 1) % len(MMB)
        return MMB[mmtoggle["n"]]

    def proj(slot, wv, col, M, rhs_fn, Wt, nk, rkeys):
        b = mmbank()
        pt = PS[b]
        for k in range(nk):
            op("pe", lambda e, k=k, pt=pt: e.matmul(pt[0:M, 0:Wt], wv[:, k, col:col + M], rhs_fn(k), start=(k == 0), stop=(k == nk - 1)),
               reads=[f"ws{slot}"] + rkeys, writes=[f"ps{b}"])
        return b, pt[0:M, 0:Wt]

    def rmsnorm(l_off_col_fn, Wt, tag, final=False):
        for c in range(8):
            op("act", lambda e, c=c: e.activation(out=xn[:, c, 0:Wt], in_=xT[:, c, 0:Wt], func=AF.Square), reads=[f"xT{c}"], writes=[f"xn{c}"])
        for c in range(8):
            op("pe", lambda e, c=c: e.matmul(PS[2][:, 0:Wt], onesNb[:], xn[:, c, 0:Wt], start=(c == 0), stop=(c == 7)),
               reads=[f"xn{c}", "onesNb"], writes=["ps2"])
        op("act", lambda e: e.activation(out=rstd[:, 0:Wt], in_=PS[2][:, 0:Wt], func=AF.Sqrt, bias=eps_t[:, 0:1]), reads=["ps2", "eps"], writes=["rstd"])
        op("dve", lambda e: e.reciprocal(out=rstd[:, 0:Wt], in_=rstd[:, 0:Wt]), reads=["rstd"], writes=["rstd"])
        for c in range(8):
            dst = xT if final else xn
            op("dve", lambda e, c=c, dst=dst: e.scalar_tensor_tensor(out=dst[:, c, 0:Wt], in0=xT[:, c, 0:Wt], scalar=l_off_col_fn(c), in1=rstd[:, 0:Wt], op0=ALU.mult, op1=ALU.mult),
               reads=[f"xT{c}", "rstd", "pv"], writes=[f"xT{c}" if final else f"xn{c}"])

    eps_t = sb("eps_t", [128, 4])
    op("dve", lambda e: e.memset(eps_t[:, 0:1], 1e-6), writes=["eps"])
    op("dve", lambda e: e.memset(eps_t[:, 1:2], 64e-5), writes=["eps"])
    op("dve", lambda e: e.memset(eps_t[:, 2:3], 1e-30), writes=["eps"])

    def ffn(l, w_in, w_out, Wt):
        for q in range(6):
            ncol = 512 if q < 5 else 256
            sg, wg = wload(w_in[l, :, 512 * q:512 * q + ncol], "in")
            su, wu = wload(w_in[l, :, DFF + 512 * q:DFF + 512 * q + ncol], "in")
            for jj in range(ncol // 128):
                j = 4 * q + jj
                bg, pg = proj(sg, wg, jj * 128, 128, lambda k: xn[:, k, 0:Wt], Wt, 8, [f"xn{k}" for k in range(8)])
                op("act", lambda e, pg=pg: e.activation(out=tmpA[:, 0:Wt], in_=pg, func=AF.Silu), reads=[f"ps{bg}"], writes=["tmpA"])
                bu, pu = proj(su, wu, jj * 128, 128, lambda k: xn[:, k, 0:Wt], Wt, 8, [f"xn{k}" for k in range(8)])
                op("dve", lambda e, pu=pu, j=j: e.tensor_tensor(out=big[:, j, 0:Wt], in0=pu, in1=tmpA[:, 0:Wt], op=ALU.mult),
                   reads=[f"ps{bu}", "tmpA"], writes=[f"big{j}"])
        for oc in range(8):
            so, wv = wload(w_out[l, :, oc * 128:(oc + 1) * 128], "out")
            b, po = proj(so, wv, 0, 128, lambda k: big[:, k, 0:Wt], Wt, 22, [f"big{k}" for k in range(22)])
            op("dve", lambda e, po=po, oc=oc: e.scalar_tensor_tensor(out=xT[:, oc, 0:Wt], in0=po, scalar=0.5, in1=xT[:, oc, 0:Wt], op0=ALU.mult, op1=ALU.add),
               reads=[f"ps{b}", f"xT{oc}"], writes=[f"xT{oc}"])

    psbt = {"n": 0}

    def shiftmix(l, pi, b, pp, nseq, nsamp, out_ap_fn, okey, last_tile):
        c0, R = PIECES[pi]
        Wt = nseq + nsamp
        psbt["n"] ^= 1
        bi = psbt["n"]
        pb = psb[bi]
        op("act", lambda e: e.activation(out=pb[0:R, 0:1], in_=carry[0:R, l, pi:pi + 1], func=AF.Copy), reads=[f"carry{l}_{pi}", "carry"], writes=[f"psb{bi}"])
        op("act", lambda e: e.activation(out=pb[0:R, 1:Wt + 1], in_=pp, func=AF.Copy), reads=[f"ps{b}"], writes=[f"psb{bi}"])
        op("act", lambda e: e.activation(out=carry[0:R, l, pi:pi + 1], in_=pb[0:R, nseq:nseq + 1], func=AF.Copy), reads=[f"psb{bi}"], writes=[f"carry{l}_{pi}"])
        mu = pv[0:R, l * PVL + O_MU + pi:l * PVL + O_MU + pi + 1]
        op("dve", lambda e: e.tensor_tensor(out=tmpB[0:R, 0:nseq], in0=pb[0:R, 0:nseq], in1=pb[0:R, 1:nseq + 1], op=ALU.subtract), reads=[f"psb{bi}"], writes=["tmpB"])
        o = out_ap_fn()
        op("dve", lambda e: e.scalar_tensor_tensor(out=o[:, 0:nseq], in0=tmpB[0:R, 0:nseq], scalar=mu, in1=pb[0:R, 1:nseq + 1], op0=ALU.mult, op1=ALU.add),
           reads=["tmpB", f"psb{bi}", "pv"], writes=[okey])
        if nsamp:
            op("dve", lambda e: e.tensor_tensor(out=tmpB[0:R, nseq:Wt], in0=sshT[0:R, l, pi, :], in1=pb[0:R, 1 + nseq:1 + Wt], op=ALU.subtract),
               reads=[f"psb{bi}", f"sshT{l}_{pi}"], writes=["tmpB"])
            op("dve", lambda e: e.scalar_tensor_tensor(out=o[:, nseq:Wt], in0=tmpB[0:R, nseq:Wt], scalar=mu, in1=pb[0:R, 1 + nseq:1 + Wt], op0=ALU.mult, op1=ALU.add),
               reads=["tmpB", f"psb{bi}", "pv"], writes=[okey])
            op("act", lambda e: e.activation(out=oshT[0:R, l, pi, :], in_=pb[0:R, 1 + nseq:1 + Wt], func=AF.Copy), reads=[f"psb{bi}"], writes=[f"oshT{l}_{pi}"])
            P.dma("sp", oshift_s[l, :, c0:c0 + R].rearrange("b f -> f b"), oshT[0:R, l, pi, :], reads=[f"oshT{l}_{pi}"], slow=True, key="out")
        if last_tile:
            P.dma("sp", oshift_p[l, c0:c0 + R].rearrange("(f o) -> f o", o=1), carry[0:R, l, pi:pi + 1], reads=[f"carry{l}_{pi}"], slow=True, key="out")

    def wkv_prep(l, c, pi_, xi, nseq, nsamp):
        B = PB[pi_]
        PSs = PS[2] if pi_ == 0 else PS[7]
        psk = "ps2" if pi_ == 0 else "ps7"
        Wt = nseq + nsamp
        nch = nseq // 64
        k_ = lambda nm: f"p{pi_}_{nm}"
        t_ = lambda nm: f"pt{pi_}_{nm}"
        PT = PTS[pi_]
        xr = xs[:, xi, 0:Wt]; xk = xs[:, 4 + xi, 0:Wt]; xv = xs[:, 8 + xi, 0:Wt]
        kr, kkk, kv = f"xs{xi}", f"xs{4 + xi}", f"xs{8 + xi}"
        cs_ = slice(c * 128, (c + 1) * 128)
        lo = B["lora"]
        P.dma("pool", lo[0:64, 0, :], ww2[l, :, cs_], writes=[k_("lora")], key=f"lora{pi_}")
        P.dma("pool", lo[0:64, 1, :], wa2[l, :, cs_], writes=[k_("lora")], key=f"lora{pi_}")
        P.dma("pool", lo[:, 2, :], wg2[l, :, cs_], writes=[k_("lora")], key=f"lora{pi_}")
        sw, cs, E2, E3, a, kk, kh, t2, f0, btn = (PT[n] for n in ["sw", "cs", "E2", "E3", "a", "kk", "kh", "t2", "f0", "btn"])
        op("pe", lambda e: e.matmul(PSs[:, 0:Wt], lo[0:64, 0, :], tlw[:, 0:Wt], start=True, stop=True), reads=["tlw", k_("lora")], writes=[psk])
        op("act", lambda e: e.activation(out=sw[:, 0:Wt], in_=PSs[:, 0:Wt], func=AF.Sigmoid, bias=pvc(l, O_W0, c)), reads=[psk, "pv"], writes=[t_("sw")])
        op("pe", lambda e: e.matmul(PSs[:, 0:Wt], lo[0:64, 1, :], lat[:, 0:Wt], start=True, stop=True), reads=["lat", k_("lora")], writes=[psk])
        op("act", lambda e: e.activation(out=a[:, 0:Wt], in_=PSs[:, 0:Wt], func=AF.Sigmoid, bias=pvc(l, O_A0, c)), reads=[psk, "pv"], writes=[t_("a")])
        op("dve", lambda e: e.tensor_tensor_scan(out=cs[:, 0:nseq], data0=cst[:, C_RST:C_RST + nseq], data1=sw[:, 0:nseq], initial=0.0, op0=ALU.mult, op1=ALU.add),
           reads=[t_("sw"), "cst"], writes=[t_("cs")])
        op("act", lambda e: e.activation(out=B["E1"][:, 0:nseq], in_=cs[:, 0:nseq], func=AF.Exp, scale=-CW), reads=[t_("cs")], writes=[k_("E1")])
        op("act", lambda e: e.activation(out=E2[:, 0:nseq], in_=cs[:, 0:nseq], func=AF.Exp, scale=CW), reads=[t_("cs")], writes=[t_("E2")])
        op("dve", lambda e: e.tensor_tensor(out=E3[:, 0:nseq], in0=cs[:, 0:nseq], in1=sw[:, 0:nseq], op=ALU.subtract), reads=[t_("cs"), t_("sw")], writes=[t_("E3")])
        op("act", lambda e: e.activation(out=E3[:, 0:nseq], in_=E3[:, 0:nseq], func=AF.Exp, scale=-CW), reads=[t_("E3")], writes=[t_("E3")])
        op("dve", lambda e: e.tensor_scalar(out=kk[:, 0:Wt], in0=xk, scalar1=pvc(l, O_KK, c), scalar2=None, op0=ALU.mult), reads=[kkk, "pv"], writes=[t_("kk")])
        op("act", lambda e: e.activation(out=f0[:, 0:Wt], in_=kk[:, 0:Wt], func=AF.Square), reads=[t_("kk")], writes=[t_("f0")])
        op("pe", lambda e: e.matmul(PSs[:, 0:Wt], obd_r, f0[:, 0:Wt], start=True, stop=True), reads=[t_("f0"), "cst"], writes=[psk])
        op("act", lambda e: e.activation(out=t2[:, 0:Wt], in_=PSs[:, 0:Wt], func=AF.Sqrt, bias=eps_t[:, 2:3]), reads=[psk, "eps"], writes=[t_("t2")])
        op("dve", lambda e: e.tensor_scalar(out=t2[:, 0:Wt], in0=t2[:, 0:Wt], scalar1=1e-12, scalar2=None, op0=ALU.max), reads=[t_("t2")], writes=[t_("t2")])
        op("dve", lambda e: e.reciprocal(out=t2[:, 0:Wt], in_=t2[:, 0:Wt]), reads=[t_("t2")], writes=[t_("t2")])
        op("dve", lambda e: e.tensor_tensor(out=kk[:, 0:Wt], in0=kk[:, 0:Wt], in1=t2[:, 0:Wt], op=ALU.mult), reads=[t_("kk"), t_("t2")], writes=[t_("kk")])
        op("dve", lambda e: e.tensor_scalar(out=t2[:, 0:Wt], in0=a[:, 0:Wt], scalar1=pvc(l, O_KA, c), scalar2=omka[:, l, c:c + 1], op0=ALU.mult, op1=ALU.add),
           reads=[t_("a"), "pv", f"omka{l}"], writes=[t_("t2")])
        op("dve", lambda e: e.tensor_tensor(out=kh[:, 0:Wt], in0=xk, in1=t2[:, 0:Wt], op=ALU.mult), reads=[kkk, t_("t2")], writes=[t_("kh")])
        op("dve", lambda e: e.tensor_tensor(out=a[:, 0:Wt], in0=kk[:, 0:Wt], in1=a[:, 0:Wt], op=ALU.mult), reads=[t_("kk"), t_("a")], writes=[t_("a")])
        op("dve", lambda e: e.scalar_tensor_tensor(out=f0[:, 0:Wt], in0=xr, scalar=pvc(l, O_RK, c), in1=kh[:, 0:Wt], op0=ALU.mult, op1=ALU.mult),
           reads=[kr, t_("kh"), "pv"], writes=[t_("f0")])
        op("pe", lambda e: e.matmul(PSs[:, 0:Wt], obd_r, f0[:, 0:Wt], start=True, stop=True), reads=[t_("f0"), "cst"], writes=[psk])
        op("dve", lambda e: e.tensor_tensor(out=B["bonus"][:, 0:Wt], in0=PSs[:, 0:Wt], in1=xv, op=ALU.mult), reads=[psk, kv], writes=[k_("bonus")])
        op("pe", lambda e: e.matmul(PSs[:, 0:Wt], lo[:, 2, :], slg[:, 0:Wt], start=True, stop=True), reads=["slg", k_("lora")], writes=[psk])
        op("act", lambda e: e.activation(out=B["g"][:, 0:Wt], in_=PSs[:, 0:Wt], func=AF.Copy), reads=[psk], writes=[k_("g")])
        v3 = lambda ap: ap.rearrange("p (c s) -> p c s", s=64)
        op("dve", lambda e: e.tensor_tensor(out=B["AR"][:, 0:nch, 0:64], in0=v3(xr[:, 0:nseq]), in1=v3(B["E1"][:, 0:nseq]), op=ALU.mult), reads=[kr, k_("E1")], writes=[k_("AR")])
        op("dve", lambda e: e.tensor_tensor(out=btn[:, 0:nseq], in0=a[:, 0:nseq], in1=E2[:, 0:nseq], op=ALU.mult), reads=[t_("a"), t_("E2")], writes=[t_("btn")])
        for h in range(2):
            ps_ = slice(64 * h, 64 * h + 64)
            op("dve", lambda e, ps_=ps_, h=h: e.scalar_tensor_tensor(out=B["AR"][ps_, 0:nch, 64 + 64 * h:128 + 64 * h], in0=v3(kk[ps_, 0:nseq]), scalar=-1.0, in1=v3(E3[ps_, 0:nseq]), op0=ALU.mult, op1=ALU.mult),
               reads=[t_("kk"), t_("E3")], writes=[k_("AR")])
            op("act", lambda e, ps_=ps_: e.activation(out=B["BDb"][ps_, 0:nch, ps_], in_=v3(btn[ps_, 0:nseq]), func=AF.Copy), reads=[t_("btn")], writes=[k_("BDb")])
            op("dve", lambda e, ps_=ps_: e.tensor_tensor(out=B["BDk"][ps_, 0:nch, ps_], in0=v3(kh[ps_, 0:nseq]), in1=v3(E2[ps_, 0:nseq]), op=ALU.mult),
               reads=[t_("kh"), t_("E2")], writes=[k_("BDk")])
            op("act", lambda e, ps_=ps_: e.activation(out=B["BDv"][ps_, 0:nch, ps_], in_=v3(xv[ps_, 0:nseq]), func=AF.Copy), reads=[kv], writes=[k_("BDv")])
        if nsamp:
            sc = slice(nseq, Wt)
            sm = B["smp"]
            op("act", lambda e: e.activation(out=sm[:, 0, :], in_=sw[:, sc], func=AF.Exp, scale=-CW), reads=[t_("sw")], writes=[k_("smp")])
            op("dve", lambda e: e.tensor_scalar(out=sm[:, 1, :], in0=kk[:, sc], scalar1=-1.0, scalar2=None, op0=ALU.mult), reads=[t_("kk")], writes=[k_("smp")])
            op("dve", lambda e: e.tensor_copy(out=sm[:, 2, :], in_=a[:, sc]), reads=[t_("a")], writes=[k_("smp")])
            op("dve", lambda e: e.tensor_copy(out=sm[:, 3, :], in_=kh[:, sc]), reads=[t_("kh")], writes=[k_("smp")])

    ROT = [3, 5, 6]
    ROTA = [3, 5, 6, 0, 1]

    def wkv_phaseA(units):
        k_ = lambda pi_, nm: f"p{pi_}_{nm}"
        mall = cst[:, C_MALL:C_MALL + 512]
        for (c, pi_, ch, u) in units:
            B = PB[pi_]; bk = ROTA[u % 5]; pt = PS[bk]
            ARc = B["AR"][:, ch, :]; BDa = B["AR"][:, ch, 64:192]; rtn = B["AR"][:, ch, 0:64]
            BDb = B["BDb"][:, ch, :]; BDk = B["BDk"][:, ch, :]
            op("pe", lambda e, pt=pt, BDk=BDk, ARc=ARc: e.matmul(pt[:, 0:192], BDk, ARc, start=True, stop=True), reads=[k_(pi_, "BDk"), k_(pi_, "AR")], writes=[f"ps{bk}"])
            op("pe", lambda e, pt=pt, BDb=BDb, rtn=rtn: e.matmul(pt[:, 192:256], BDb, rtn, start=True, stop=True), reads=[k_(pi_, "BDb"), k_(pi_, "AR")], writes=[f"ps{bk}"])
            op("pe", lambda e, pt=pt, BDa=BDa, BDb=BDb: e.matmul(pt[:, 256:384], BDa, BDb, start=True, stop=True), reads=[k_(pi_, "BDb"), k_(pi_, "AR")], writes=[f"ps{bk}"])
            op("pe", lambda e, pt=pt, BDa=BDa, BDb=BDb: e.matmul(pt[:, 384:512], BDb, BDa, start=True, stop=True), reads=[k_(pi_, "BDb"), k_(pi_, "AR")], writes=[f"ps{bk}"])
            op("dve", lambda e, pt=pt, u=u: e.tensor_tensor(out=UB[:, u, 0:512], in0=pt[:, 0:512], in1=mall, op=ALU.mult), reads=[f"ps{bk}", "cst"], writes=[f"ub{u}"])
        TB = [(PS[4][:, 0:384], "ps4"), (PS[2][:, 0:192].bitcast(BF16), "ps2"), (PS[7][:, 0:192].bitcast(BF16), "ps7")]
        for (c, pi_, ch, u) in units:
            B = PB[pi_]
            tb, tk = TB[u % 3]
            for i, nm in enumerate(["BDb", "BDk", "BDv"]):
                op("pe", lambda e, i=i, tb=tb, src=B[nm][:, ch, :]: e.transpose(tb[:, 128 * i:128 * i + 128], src, identb[:]), reads=[k_(pi_, nm), "identb"], writes=[tk])
            op("act", lambda e, u=u, tb=tb: e.activation(out=UB[:, u, 640:1024], in_=tb, func=AF.Copy), reads=[tk], writes=[f"ub{u}"])
            op("dve", lambda e, u=u: e.tensor_copy(out=UB[:, u, 512:640], in_=identb[:]), reads=["identb"], writes=[f"ub{u}"])

        def level(u, bk):
            pt = PS[bk]
            N_ = UB[:, u, 256:384]; Nt_ = UB[:, u, 384:512]; Q_ = UB[:, u, 512:640]
            op("pe", lambda e: e.matmul(pt[:, 0:128], Nt_, N_, start=True, stop=True), reads=[f"ub{u}"], writes=[f"ps{bk}"])
            op("pe", lambda e: e.matmul(pt[:, 128:256], N_, Nt_, start=True, stop=True), reads=[f"ub{u}"], writes=[f"ps{bk}"])
            op("pe", lambda e: e.matmul(pt[:, 256:384], identb[:], Q_, start=True, stop=False), reads=[f"ub{u}", "identb"], writes=[f"ps{bk}"])
            op("pe", lambda e: e.matmul(pt[:, 256:384], N_, Q_, start=False, stop=True), reads=[f"ub{u}"], writes=[f"ps{bk}"])
            if u % 2 == 0:
                op("act", lambda e: e.activation(out=UB[:, u, 256:640], in_=pt[:, 0:384], func=AF.Copy), reads=[f"ps{bk}"], writes=[f"ub{u}"])
            else:
                op("dve", lambda e: e.tensor_copy(out=UB[:, u, 256:640], in_=pt[:, 0:384]), reads=[f"ps{bk}"], writes=[f"ub{u}"])

        for lev in range(1, 7):
            for (c, pi_, ch, u) in units:
                level(u, ROTA[u % 5])

    def wkv_phaseB(l, ulist):
        k_ = lambda pi_, nm: f"p{pi_}_{nm}"
        info = []
        for (c, pi_, ch, u) in ulist:
            B = PB[pi_]; bk = ROT[pi_ % 3]
            info.append(dict(c=c, pi_=pi_, ch=ch, u=u, B=B, bk=bk, pt=PS[bk], sbf=Sb[:, l, c, :], sf=Sf[:, l, c, :], skb=f"Sb{l}_{c}", skf=f"Sf{l}_{c}",
                             BDa=B["AR"][:, ch, 64:192], rtn=B["AR"][:, ch, 0:64], X=B["U"][:, 0, :], Ubd=B["U"][:, 1, :]))
        for d in info:
            u = d["u"]
            op("pe", lambda e, d=d: e.matmul(d["pt"][:, 0:128], d["BDa"], d["sbf"], start=True, stop=False), reads=[k_(d["pi_"], "AR"), d["skb"], "Sb"], writes=[f"ps{d['bk']}"])
            op("pe", lambda e, d=d, u=u: e.matmul(d["pt"][:, 0:128], UB[:, u, 64:192], UB[:, u, 896:1024], start=False, stop=True), reads=[f"ub{u}"], writes=[f"ps{d['bk']}"])
        for d in info:
            op("act", lambda e, d=d: e.activation(out=d["X"], in_=d["pt"][:, 0:128], func=AF.Copy), reads=[f"ps{d['bk']}"], writes=[k_(d["pi_"], "X")])
        for d in info:
            u = d["u"]
            op("pe", lambda e, d=d, u=u: e.matmul(d["pt"][:, 128:256], UB[:, u, 512:640], d["X"], start=True, stop=True), reads=[f"ub{u}", k_(d["pi_"], "X")], writes=[f"ps{d['bk']}"])
        for d in info:
            op("dve", lambda e, d=d: e.tensor_copy(out=d["Ubd"], in_=d["pt"][:, 128:256]), reads=[f"ps{d['bk']}"], writes=[k_(d["pi_"], "Ub")])
        for d in info:
            u = d["u"]; pi_ = d["pi_"]
            op("pe", lambda e, d=d: e.matmul(d["pt"][:, 256:320], d["sbf"], d["rtn"], start=True, stop=False), reads=[d["skb"], "Sb", k_(pi_, "AR")], writes=[f"ps{d['bk']}"])
            op("pe", lambda e, d=d, u=u: e.matmul(d["pt"][:, 256:320], d["Ubd"], UB[:, u, 192:256], start=False, stop=False), reads=[k_(pi_, "Ub"), f"ub{u}"], writes=[f"ps{d['bk']}"])
            op("pe", lambda e, d=d, u=u: e.matmul(d["pt"][:, 256:320], UB[:, u, 896:1024], UB[:, u, 0:64], start=False, stop=True), reads=[f"ub{u}"], writes=[f"ps{d['bk']}"])
        for d in info:
            pi_ = d["pi_"]; ch = d["ch"]; B = d["B"]
            op("act", lambda e, d=d, B=B, ch=ch: e.activation(out=B["yT"][:, 64 * ch:64 * ch + 64], in_=d["pt"][:, 256:320], func=AF.Copy), reads=[f"ps{d['bk']}"], writes=[k_(pi_, "yT")])
        for d in info:
            u = d["u"]; pi_ = d["pi_"]
            op("pe", lambda e, d=d, u=u: e.matmul(d["pt"][:, 320:448], UB[:, u, 640:768], d["Ubd"], start=True, stop=False), reads=[f"ub{u}", k_(pi_, "Ub")], writes=[f"ps{d['bk']}"])
            op("pe", lambda e, d=d, u=u: e.matmul(d["pt"][:, 320:448], UB[:, u, 768:896], UB[:, u, 896:1024], start=False, stop=True), reads=[f"ub{u}"], writes=[f"ps{d['bk']}"])
        for d in info:
            pi_ = d["pi_"]; ch = d["ch"]; B = d["B"]
            pc = B["E1"][:, 64 * ch + 63:64 * ch + 64]
            op("dve", lambda e, d=d, pc=pc: e.tensor_scalar(out=d["sf"], in0=d["sf"], scalar1=pc, scalar2=None, op0=ALU.mult), reads=[d["skf"], "Sf", k_(pi_, "E1")], writes=[d["skf"]])
            op("dve", lambda e, d=d, pc=pc: e.scalar_tensor_tensor(out=d["sf"], in0=d["pt"][:, 320:448], scalar=pc, in1=d["sf"], op0=ALU.mult, op1=ALU.add), reads=[f"ps{d['bk']}", d["skf"], k_(pi_, "E1")], writes=[d["skf"]])
            op("act", lambda e, d=d: e.activation(out=d["sbf"], in_=d["sf"], func=AF.Copy), reads=[d["skf"]], writes=[d["skb"]])

    BKS = [3, 5, 6, 0, 1]

    def wkv_sample(l, c, pi_, xi, nseq):
        B = PB[pi_]
        k_ = lambda nm: f"p{pi_}_{nm}"
        sc0 = nseq
        sm = B["smp"]
        kr, kv = f"xs{xi}", f"xs{8 + xi}"
        SK = ["sst0", "sst1", "dxp0", "dxp1", "st1", "st20", "st21"]
        op("dve", lambda e: e.memset(fence_t[:, 0:1], 0.0), writes=SK + UBK)
        H = 8

        def half(hf):
            bs = slice(H * hf, H * hf + H)
            scs = slice(sc0 + H * hf, sc0 + H * hf + H)
            sst_h = SBUFS["sst"][hf]; st2_h = SBUFS["st2"][hf]; st1 = SBUFS["st1"]
            qs = [(sm[:, 1, bs], [k_("smp")]), (sm[:, 0, bs], [k_("smp")]), (sm[:, 2, bs], [k_("smp")]), (sm[:, 3, bs], [k_("smp")]), (xs[:, xi, scs], [kr])]
            for qi, (q_ap, qk) in enumerate(qs):
                dx = SBUFS["dxp"][qi % 2]
                op("dve", lambda e, dx=dx, q_ap=q_ap: e.tensor_tensor(out=dx, in0=q_ap.unsqueeze(2).to_broadcast([128, H, 64]), in1=i64s.unsqueeze(1).to_broadcast([128, H, 64]), op=ALU.mult),
                   reads=qk + ["cst"], writes=[f"dxp{qi % 2}"])
                bk = BKS[qi]
                op("pe", lambda e, dx=dx, bk=bk: e.matmul(PS[bk][:, :], obd_f, dx.rearrange("p b k -> p (b k)"), start=True, stop=True), reads=[f"dxp{qi % 2}", "cst"], writes=[f"ps{bk}"])
            bcv = lambda qi: PS[BKS[qi]][:, :].rearrange("p (b k) -> p b k", k=64)
            bck = lambda qi: f"ps{BKS[qi]}"
            op("dve", lambda e: e.tensor_tensor(out=st1, in0=bcv(0), in1=sst_h, op=ALU.mult), reads=[bck(0), f"sst{hf}"], writes=["st1"])
            op("dve", lambda e: e.tensor_reduce(out=sa_t[:, bs], in_=st1, axis=AX.X, op=ALU.add), reads=["st1"], writes=["sa_t"])
            op("dve", lambda e: e.tensor_tensor(out=st2_h, in0=bcv(1), in1=sst_h, op=ALU.mult), reads=[bck(1), f"sst{hf}"], writes=[f"st2{hf}"])
            op("dve", lambda e: e.tensor_tensor(out=st1, in0=bcv(2), in1=sa_t[:, bs].unsqueeze(2).to_broadcast([128, H, 64]), op=ALU.mult), reads=[bck(2), "sa_t"], writes=["st1"])
            op("dve", lambda e: e.tensor_tensor(out=st2_h, in0=st2_h, in1=st1, op=ALU.add), reads=["st1", f"st2{hf}"], writes=[f"st2{hf}"])
            op("dve", lambda e: e.tensor_tensor(out=st1, in0=bcv(3), in1=xs[:, 8 + xi, scs].unsqueeze(2).to_broadcast([128, H, 64]), op=ALU.mult), reads=[bck(3), kv], writes=["st1"])
            op("dve", lambda e: e.tensor_tensor(out=st2_h, in0=st2_h, in1=st1, op=ALU.add), reads=["st1", f"st2{hf}"], writes=[f"st2{hf}"])
            P.dma("sp", owkv_s[l, bs, 2 * c:2 * c + 2].rearrange("b h v k -> (h v) b k"), st2_h, reads=[f"st2{hf}"])
            op("dve", lambda e: e.tensor_tensor(out=st1, in0=bcv(4), in1=st2_h, op=ALU.mult), reads=[bck(4), f"st2{hf}"], writes=["st1"])
            op("dve", lambda e: e.tensor_reduce(out=ys_t[:, bs], in_=st1, axis=AX.X, op=ALU.add), reads=["st1"], writes=["ys_t"])

        for hf in range(2):
            P.dma("sp", SBUFS["sst"][hf], swkv[l, H * hf:H * hf + H, 2 * c:2 * c + 2].rearrange("b h v k -> (h v) b k"), writes=[f"sst{hf}"])
        half(0)
        half(1)
        op("dve", lambda e: e.tensor_copy(out=B["yT"][:, sc0:sc0 + NSAMP], in_=ys_t[:]), reads=["ys_t"], writes=[k_("yT")])
        op("dve", lambda e: e.memset(fence_t[:, 1:2], 0.0), writes=SK + UBK)

    def wkv_post(l, c, pi_, xi, Wt):
        B = PB[pi_]
        PSs = PS[2] if pi_ == 0 else PS[7]
        psk = "ps2" if pi_ == 0 else "ps7"
        PT = PTS[pi_]
        k_ = lambda nm: f"p{pi_}_{nm}"
        yTf = B["yT"][:, 0:Wt]
        op("pe", lambda e: e.matmul(PSs[:, 0:Wt], obd64_r, B["yT"][:, 0:Wt], start=True, stop=True), reads=[k_("yT"), "cst"], writes=[psk])
        op("dve", lambda e: e.tensor_tensor(out=PT["kk"][:, 0:Wt], in0=yTf, in1=PSs[:, 0:Wt], op=ALU.subtract), reads=[psk, k_("yT")], writes=[f"pt{pi_}_kk"])
        op("act", lambda e: e.activation(out=B["f0"][:, 0:Wt], in_=PT["kk"][:, 0:Wt], func=AF.Square), reads=[f"pt{pi_}_kk"], writes=[k_("f0")])
        op("pe", lambda e: e.matmul(PSs[:, 0:Wt], obd64_r, B["f0"][:, 0:Wt], start=True, stop=True), reads=[k_("f0"), "cst"], writes=[psk])
        op("act", lambda e: e.activation(out=PT["kh"][:, 0:Wt], in_=PSs[:, 0:Wt], func=AF.Sqrt, bias=eps_t[:, 1:2]), reads=[psk, "eps"], writes=[f"pt{pi_}_kh"])
        op("dve", lambda e: e.reciprocal(out=PT["kh"][:, 0:Wt], in_=PT["kh"][:, 0:Wt]), reads=[f"pt{pi_}_kh"], writes=[f"pt{pi_}_kh"])
        op("dve", lambda e: e.tensor_tensor(out=PT["kk"][:, 0:Wt], in0=PT["kk"][:, 0:Wt], in1=PT["kh"][:, 0:Wt], op=ALU.mult), reads=[f"pt{pi_}_kk", f"pt{pi_}_kh"], writes=[f"pt{pi_}_kk"])
        op("dve", lambda e: e.tensor_scalar(out=PT["kk"][:, 0:Wt], in0=PT["kk"][:, 0:Wt], scalar1=pvc(l, O_LNW, c), scalar2=pvc(l, O_LNB, c), op0=ALU.mult, op1=ALU.add),
           reads=[f"pt{pi_}_kk", "pv"], writes=[f"pt{pi_}_kk"])
        op("dve", lambda e: e.tensor_tensor(out=PT["kk"][:, 0:Wt], in0=PT["kk"][:, 0:Wt], in1=B["bonus"][:, 0:Wt], op=ALU.add), reads=[f"pt{pi_}_kk", k_("bonus")], writes=[f"pt{pi_}_kk"])
        op("dve", lambda e: e.tensor_tensor(out=PT["kk"][:, 0:Wt], in0=PT["kk"][:, 0:Wt], in1=B["g"][:, 0:Wt], op=ALU.mult), reads=[f"pt{pi_}_kk", k_("g")], writes=[f"pt{pi_}_kk"])
        op("dve", lambda e: e.tensor_tensor(out=PT["kk"][:, 0:Wt], in0=PT["kk"][:, 0:Wt], in1=sga[xi][:, 0:Wt], op=ALU.mult), reads=[f"pt{pi_}_kk", f"sga{xi}"], writes=[f"pt{pi_}_kk"])
        op("dve", lambda e: e.tensor_tensor(out=big[:, c, 0:Wt], in0=PT["kk"][:, 0:Wt], in1=ybuf[xi][:, 0:Wt], op=ALU.add), reads=[f"pt{pi_}_kk", f"ybuf{xi}"], writes=[f"big{c}"])

    def mixer(l, ti, nseq, nsamp):
        Wt = nseq + nsamp
        nch = nseq // 64
        last = (ti == len(TILES) - 1)
        xnk = [f"xn{k}" for k in range(8)]
        rhs = lambda k: xn[:, k, 0:Wt]
        s, wv = wload(wi[l, :, 3072:3328], "in")
        b, pp = proj(s, wv, 0, 64, rhs, Wt, 8, xnk)
        shiftmix(l, 24, b, pp, nseq, nsamp, lambda: tmpA[0:64, 0:Wt], "tmpA", last)
        op("act", lambda e: e.activation(out=tlw[:, 0:Wt], in_=tmpA[0:64, 0:Wt], func=AF.Tanh), reads=["tmpA"], writes=["tlw"])
        b, pp = proj(s, wv, 64, 64, rhs, Wt, 8, xnk)
        shiftmix(l, 25, b, pp, nseq, nsamp, lambda: lat[:, 0:Wt], "lat", last)
        b, pp = proj(s, wv, 128, 128, rhs, Wt, 8, xnk)
        shiftmix(l, 26, b, pp, nseq, nsamp, lambda: tmpA[:, 0:Wt], "tmpA", last)
        op("act", lambda e: e.activation(out=slg[:, 0:Wt], in_=tmpA[:, 0:Wt], func=AF.Sigmoid), reads=["tmpA"], writes=["slg"])
        for q in range(2):
            for ty in range(3):
                s, wv = wload(wi[l, :, 1024 * ty + 512 * q:1024 * ty + 512 * q + 512], "in")
                for pp_ in range(4):
                    b, pp = proj(s, wv, 128 * pp_, 128, rhs, Wt, 8, xnk)
                    bi = 4 * ty + pp_
                    shiftmix(l, 8 * ty + 4 * q + pp_, b, pp, nseq, nsamp, lambda bi=bi: xs[:, bi, 0:Wt], f"xs{bi}", last)
            rec_lists = []
            for xi, pi_ in [(i, i) for i in range(NPF)]:
                P.begin()
                wkv_prep(l, 4 * q + xi, pi_, xi, nseq, nsamp)
                rec_lists.append(P.end())
            P.begin()
            CB = DSH
            s, wv = wload(wi[l, :, CB + 2048 + 512 * q:CB + 2048 + 512 * q + 512], "in")
            s2, wv2 = wload(wi[l, :, CB + 1024 + 512 * q:CB + 1024 + 512 * q + 512], "in")
            for pp_ in range(4):
                c = 4 * q + pp_
                b, pp = proj(s, wv, 128 * pp_, 128, rhs, Wt, 8, xnk)
                op("act", lambda e, pp=pp: e.activation(out=tmpA[:, 0:Wt], in_=pp, func=AF.Copy), reads=[f"ps{b}"], writes=["tmpA"])
                b2, pp2 = proj(s2, wv2, 128 * pp_, 128, rhs, Wt, 8, xnk)
                ub = ubuf[pp_]
                op("act", lambda e, ub=ub, c=c: e.activation(out=ub[:, 0:2], in_=ucarry[:, l, c, :], func=AF.Copy), reads=[f"ucarry{l}_{c}", "ucarry"], writes=[f"ubuf{pp_}"])
                op("dve", lambda e, ub=ub, pp2=pp2: e.tensor_tensor(out=ub[:, 2:2 + Wt], in0=pp2, in1=tmpA[:, 0:Wt], op=ALU.mult), reads=[f"ps{b2}", "tmpA"], writes=[f"ubuf{pp_}"])
                op("act", lambda e, ub=ub, c=c: e.activation(out=ucarry[:, l, c, :], in_=ub[:, nseq:nseq + 2], func=AF.Copy), reads=[f"ubuf{pp_}"], writes=[f"ucarry{l}_{c}"])
                if last:
                    P.dma("sp", oconv_p[l, :, c * 128:(c + 1) * 128].rearrange("j f -> f j"), ucarry[:, l, c, :], reads=[f"ucarry{l}_{c}"], slow=True, key="out")
                yb = ybuf[pp_]
                cw = lambda j, c=c: pvc(l, O_CW + 8 * j, c)
                op("dve", lambda e, ub=ub, yb=yb, cw=cw: e.tensor_scalar(out=yb[:, 0:nseq], in0=ub[:, 0:nseq], scalar1=cw(0), scalar2=None, op0=ALU.mult), reads=[f"ubuf{pp_}", "pv"], writes=[f"ybuf{pp_}"])
                op("dve", lambda e, ub=ub, yb=yb, cw=cw: e.scalar_tensor_tensor(out=yb[:, 0:nseq], in0=ub[:, 1:nseq + 1], scalar=cw(1), in1=yb[:, 0:nseq], op0=ALU.mult, op1=ALU.add), reads=[f"ubuf{pp_}", "pv", f"ybuf{pp_}"], writes=[f"ybuf{pp_}"])
                op("dve", lambda e, ub=ub, yb=yb, cw=cw: e.scalar_tensor_tensor(out=yb[:, 0:nseq], in0=ub[:, 2:nseq + 2], scalar=cw(2), in1=yb[:, 0:nseq], op0=ALU.mult, op1=ALU.add), reads=[f"ubuf{pp_}", "pv", f"ybuf{pp_}"], writes=[f"ybuf{pp_}"])
                if nsamp:
                    sc = slice(nseq, Wt)
                    usl = ub[:, 2 + nseq:2 + Wt]
                    op("dve", lambda e, yb=yb, cw=cw, c=c: e.tensor_scalar(out=yb[:, sc], in0=scvT[:, l, c, 0, :], scalar1=cw(0), scalar2=None, op0=ALU.mult), reads=[f"scvT{l}_{c}_0", "pv"], writes=[f"ybuf{pp_}"])
                    op("dve", lambda e, yb=yb, cw=cw, c=c: e.scalar_tensor_tensor(out=yb[:, sc], in0=scvT[:, l, c, 1, :], scalar=cw(1), in1=yb[:, sc], op0=ALU.mult, op1=ALU.add), reads=[f"scvT{l}_{c}_1", "pv", f"ybuf{pp_}"], writes=[f"ybuf{pp_}"])
                    op("dve", lambda e, yb=yb, cw=cw, usl=usl: e.scalar_tensor_tensor(out=yb[:, sc], in0=usl, scalar=cw(2), in1=yb[:, sc], op0=ALU.mult, op1=ALU.add), reads=[f"ubuf{pp_}", "pv", f"ybuf{pp_}"], writes=[f"ybuf{pp_}"])
                    op("act", lambda e, usl=usl, c=c: e.activation(out=ocvT[:, l, c, :], in_=usl, func=AF.Copy), reads=[f"ubuf{pp_}"], writes=[f"ocvT{l}_{c}"])
                    P.dma("sp", oconv_s[l, :, 0, c * 128:(c + 1) * 128].rearrange("b f -> f b"), scvT[:, l, c, 1, :], reads=[f"scvT{l}_{c}_1"], slow=True, key="out")
                    P.dma("sp", oconv_s[l, :, 1, c * 128:(c + 1) * 128].rearrange("b f -> f b"), ocvT[:, l, c, :], reads=[f"ocvT{l}_{c}"], slow=True, key="out")
            s, wv = wload(wi[l, :, CB + 512 * q:CB + 512 * q + 512], "in")
            for pp_ in range(4):
                b, pp = proj(s, wv, 128 * pp_, 128, rhs, Wt, 8, xnk)
                op("dve", lambda e, pp=pp, pp_=pp_: e.tensor_tensor(out=ybuf[pp_][:, 0:Wt], in0=pp, in1=ybuf[pp_][:, 0:Wt], op=ALU.mult), reads=[f"ps{b}", f"ybuf{pp_}"], writes=[f"ybuf{pp_}"])
            s, wv = wload(wi[l, :, CB + 3072 + 1024 + 512 * q:CB + 3072 + 1024 + 512 * q + 512], "in")
            for pp_ in range(4):
                b, pp = proj(s, wv, 128 * pp_, 128, rhs, Wt, 8, xnk)
                op("act", lambda e, pp=pp: e.activation(out=tmpA[:, 0:Wt], in_=pp, func=AF.Sigmoid), reads=[f"ps{b}"], writes=["tmpA"])
                op("dve", lambda e, pp_=pp_: e.tensor_tensor(out=ybuf[pp_][:, 0:Wt], in0=tmpA[:, 0:Wt], in1=ybuf[pp_][:, 0:Wt], op=ALU.mult), reads=["tmpA", f"ybuf{pp_}"], writes=[f"ybuf{pp_}"])
            s, wv = wload(wi[l, :, CB + 3072 + 512 * q:CB + 3072 + 512 * q + 512], "in")
            for pp_ in range(4):
                b, pp = proj(s, wv, 128 * pp_, 128, rhs, Wt, 8, xnk)
                op("act", lambda e, pp=pp, pp_=pp_: e.activation(out=sga[pp_][:, 0:Wt], in_=pp, func=AF.Sigmoid), reads=[f"ps{b}"], writes=[f"sga{pp_}"])
            rec_lists.append(P.end())
            P.play(rec_lists)
            for sub in range(4 // NPF):
                prs = [(NPF * sub + i, i) for i in range(NPF)]
                if sub > 0:
                    rl = []
                    for xi, pi_ in prs:
                        P.begin()
                        wkv_prep(l, 4 * q + xi, pi_, xi, nseq, nsamp)
                        rl.append(P.end())
                    P.play(rl)
                units = [(4 * q + xi, pi_, ch, pi_ * NCH + ch) for xi, pi_ in prs for ch in range(nch)]
                wkv_phaseA(units)
                for ch in range(nch):
                    wkv_phaseB(l, [un for un in units if un[2] == ch])
                for xi, pi_ in prs:
                    c = 4 * q + xi
                    if nsamp:
                        wkv_sample(l, c, pi_, xi, nseq)
                    if last:
                        op("pe", lambda e, c=c: e.transpose(PS[7][:, 0:128], Sf[:, l, c, :], ident), reads=[f"Sf{l}_{c}", "Sf", "cst"], writes=["ps7"])
                        op("act", lambda e: e.activation(out=fin[:], in_=PS[7][:, 0:128], func=AF.Copy), reads=["ps7"], writes=["fin"])
                        for h in range(2):
                            P.dma("sp", owkv_p[l, 2 * c + h], fin[64 * h:64 * h + 64, 64 * h:64 * h + 64], reads=["fin"])
                rl = []
                for xi, pi_ in prs:
                    P.begin()
                    wkv_post(l, 4 * q + xi, pi_, xi, Wt)
                    rl.append(P.end())
                P.play(rl)
        for hh in range(2):
            s, wv = wload(wo[l, :, 512 * hh:512 * hh + 512], "in")
            for pp_ in range(4):
                oc = 4 * hh + pp_
                b, pp = proj(s, wv, 128 * pp_, 128, lambda k: big[:, k, 0:Wt], Wt, 8, [f"big{k}" for k in range(8)])
                op("dve", lambda e, pp=pp, oc=oc: e.tensor_tensor(out=xT[:, oc, 0:Wt], in0=pp, in1=xT[:, oc, 0:Wt], op=ALU.add), reads=[f"ps{b}", f"xT{oc}"], writes=[f"xT{oc}"])
        if ti == 0:
            for oc in range(8):
                op("dve", lambda e, oc=oc: e.memset(xT[:, oc, 0:48], 0.0), writes=[f"xT{oc}"])

    def load_rows(blk_rows):
        for (r0, nr, src) in blk_rows:
            if src is None:
                op("dve", lambda e, r0=r0, nr=nr: e.memset(xtok[r0:r0 + nr, :], 0.0), writes=["xtok"])
            else:
                P.dma("sp", xtok[r0:r0 + nr, :], src, writes=["xtok"])

    for ti, (Wt, nch, nsamp) in enumerate(TILES):
        wstate["j"] = 0
        wstate["tile"] = ti
        nseq = nch * 64
        nblk = (Wt + 127) // 128
        for j in range(nblk):
            nr = min(128, Wt - 128 * j)
            if ti == 0 and j == 0:
                rows = [(0, 64, None), (48, 16, meta[:, :]), (64, 64, xp[0:64, :])]
                rows = [(0, 32, None), (32, 32, None), (48, 16, meta[:, :]), (64, 64, xp[0:64, :])]
            else:
                i0 = TW * ti + 128 * j - 64
                ns = max(0, min(128, nseq - 128 * j))
                rows = []
                if ns:
                    rows.append((0, ns, xp[i0:i0 + ns, :]))
                if nsamp and ns < nr:
                    rows.append((ns, NSAMP, xsm[:, :]))
            load_rows(rows)
            for half in range(2):
                for cc in range(4):
                    c = 4 * half + cc
                    op("pe", lambda e, c=c, cc=cc, nr=nr: e.transpose(PS[7][:, 128 * cc:128 * cc + nr], xtok[0:nr, c * 128:(c + 1) * 128], ident[0:nr, 0:nr]),
                       reads=["xtok", "cst"], writes=["ps7"])
                for cc in range(4):
                    c = 4 * half + cc
                    op("act", lambda e, c=c, cc=cc, nr=nr, j=j: e.activation(out=xT[:, c, 128 * j:128 * j + nr], in_=PS[7][:, 128 * cc:128 * cc + nr], func=AF.Copy),
                       reads=["ps7"], writes=[f"xT{c}"])
        for l in range(NL):
            rmsnorm(lambda c, l=l: pvc(l, O_NF1, c), Wt, "f1")
            ffn(l, f1i, f1o, Wt)
            rmsnorm(lambda c, l=l: pvc(l, O_NMIX, c), Wt, "mix")
            mixer(l, ti, nseq, nsamp)
            rmsnorm(lambda c, l=l: pvc(l, O_NF2, c), Wt, "f2")
            ffn(l, f2i, f2o, Wt)
        rmsnorm(lambda c: pv[:, 2 * PVL + c:2 * PVL + c + 1], Wt, "fin", final=True)
        for j in range(nblk):
            nr = min(128, Wt - 128 * j)
            for half in range(2):
                for cc in range(4):
                    c = 4 * half + cc
                    op("pe", lambda e, c=c, cc=cc, nr=nr, j=j: e.transpose(PS[7][0:nr, 128 * cc:128 * cc + 128], xT[:, c, 128 * j:128 * j + nr], ident),
                       reads=[f"xT{c}", "cst"], writes=["ps7"])
                op("act", lambda e, half=half, nr=nr: e.activation(out=ytok[0:nr, 512 * half:512 * half + 512], in_=PS[7][0:nr, :], func=AF.Copy), reads=["ps7"], writes=["xtok"])
            if ti == 0 and j == 0:
                P.dma("sp", yp[0:64, :], ytok[64:128, :], reads=["xtok"])
            else:
                i0 = TW * ti + 128 * j - 64
                ns = max(0, min(128, nseq - 128 * j))
                if ns:
                    P.dma("sp", yp[i0:i0 + ns, :], ytok[0:ns, :], reads=["xtok"])
                if nsamp and ns < nr:
                    P.dma("sp", ys[:, :], ytok[ns:ns + NSAMP, :], reads=["xtok"])
    P.emit()
    es.close()
    return nc


def _fm(v):
    return np.ascontiguousarray(np.asarray(v, np.float32).reshape(8, 128).T)


def _consts():
    c = np.zeros((128, NCST), np.float32)
    p = np.arange(128)
    c[:, C_ID:C_ID + 128] = np.eye(128, dtype=np.float32)
    c[:, C_ON:C_ON + 128] = 1.0 / 1024.0
    same = (p[:, None] // 64) == (p[None, :] // 64)
    c[:, C_OBD:C_OBD + 128] = same
    c[:, C_OBD64:C_OBD64 + 128] = same / 64.0
    msu = same & ((p[None, :] % 64) > (p[:, None] % 64))
    msl = same & ((p[None, :] % 64) < (p[:, None] % 64))
    t = np.arange(64)
    miu = t[None, :] >= (p[:, None] % 64)
    c[:, C_MALL:C_MALL + 64] = miu
    c[:, C_MALL + 64:C_MALL + 192] = msu
    c[:, C_MALL + 192:C_MALL + 256] = miu
    c[:, C_MALL + 256:C_MALL + 384] = msl
    c[:, C_MALL + 384:C_MALL + 512] = msu
    c[:, C_I64:C_I64 + 64] = t[None, :] == (p[:, None] % 64)
    r = np.ones(320, np.float32)
    r[::64] = 0.0
    c[:, C_RST:C_RST + 320] = r[None, :]
    return c


_CACHE = {}


def make_maps(x_prompt, x_sample, state_wkv, state_shift, state_conv, meta_tokens,
              norm_ffn1, ffn1_w_in, ffn1_w_out, norm_mix, w_in, mu_shift, w0, w_w2, a0,
              w_a2, w_g2, k_k, k_a, r_k, lnx_w, lnx_b, conv_w, w_o, norm_ffn2,
              ffn2_w_in, ffn2_w_out, norm_final):
    f = lambda a: np.ascontiguousarray(np.asarray(a, dtype=np.float32))
    x_prompt = f(x_prompt); x_sample = f(x_sample); state_wkv = f(state_wkv); state_shift = f(state_shift); state_conv = f(state_conv)
    pv = np.zeros((128, NV), np.float32)
    for l in range(NL):
        o = l * PVL
        pv[:, o + O_NF1:o + O_NF1 + 8] = _fm(norm_ffn1[l]); pv[:, o + O_NMIX:o + O_NMIX + 8] = _fm(norm_mix[l]); pv[:, o + O_NF2:o + O_NF2 + 8] = _fm(norm_ffn2[l])
        mu = np.asarray(mu_shift[l], np.float32)
        for pi, (c0, R) in enumerate(PIECES):
            pv[0:R, o + O_MU + pi] = mu[c0:c0 + R]
        pv[:, o + O_W0:o + O_W0 + 8] = _fm(w0[l]); pv[:, o + O_A0:o + O_A0 + 8] = _fm(a0[l]); pv[:, o + O_KK:o + O_KK + 8] = _fm(k_k[l]); pv[:, o + O_KA:o + O_KA + 8] = _fm(k_a[l])
        pv[:, o + O_RK:o + O_RK + 8] = _fm(np.asarray(r_k[l]).reshape(-1)); pv[:, o + O_LNW:o + O_LNW + 8] = _fm(lnx_w[l]); pv[:, o + O_LNB:o + O_LNB + 8] = _fm(lnx_b[l])
        for j in range(3):
            pv[:, o + O_CW + 8 * j:o + O_CW + 8 * j + 8] = _fm(np.asarray(conv_w[l])[j])
    pv[:, 2 * PVL:2 * PVL + 8] = _fm(norm_final)
    cst = _consts()
    shared = {"meta": f(meta_tokens), "f1i": f(ffn1_w_in), "f1o": f(ffn1_w_out), "wi": f(w_in), "wo": f(w_o), "f2i": f(ffn2_w_in), "f2o": f(ffn2_w_out),
              "ww2": f(w_w2), "wa2": f(w_a2), "wg2": f(w_g2), "pv": pv, "cst": cst}
    in_maps = []
    for b in range(8):
        m = dict(shared)
        m["xp"] = x_prompt[b]
        m["xsm"] = np.ascontiguousarray(x_sample[16 * b:16 * b + 16, 0, :])
        m["swkv"] = np.ascontiguousarray(state_wkv[:, 16 * b:16 * b + 16])
        m["sshift"] = np.ascontiguousarray(state_shift[:, 16 * b:16 * b + 16])
        m["sconv"] = np.ascontiguousarray(state_conv[:, 16 * b:16 * b + 16])
        in_maps.append(m)
    return in_maps


def gather(R):
    n = len(R)
    y_prompt = np.stack([R[b]["yp"] for b in range(n)], 0)
    y_sample = np.concatenate([R[b]["ys"] for b in range(n)], 0)[:, None, :]
    p_wkv = np.stack([R[b]["owkv_p"] for b in range(n)], 1)
    p_shift = np.stack([R[b]["oshift_p"] for b in range(n)], 1)
    p_conv = np.stack([R[b]["oconv_p"] for b in range(n)], 1)
    s_wkv = np.concatenate([R[b]["owkv_s"] for b in range(n)], 1)
    s_shift = np.concatenate([R[b]["oshift_s"] for b in range(n)], 1)
    s_conv = np.concatenate([R[b]["oconv_s"] for b in range(n)], 1)
    return tuple(np.ascontiguousarray(a, dtype=np.float32) for a in (y_prompt, y_sample, p_wkv, p_shift, p_conv, s_wkv, s_shift, s_conv))


def kernel(**inputs):
    in_maps = make_maps(**inputs)
    if "nc" not in _CACHE:
        _CACHE["nc"] = build_program()
    res = run_bass_kernel_spmd(_CACHE["nc"], in_maps, core_ids=list(range(8)))
    return gather(res.results)
```
